# Optimizing a Trainium2 kernel written in Bass

```python
import jax
import jax.numpy as jnp
from jax import lax
import numpy as np

D_MODEL = 1024
BATCH = 8
SEQ = 2048
DEPTH = 1

GRID_W = 64
CTX_LEN = 256
FNET_DIM = 512
FNET_GROUPS = 4
FNET_GROUP_DIM = FNET_DIM // FNET_GROUPS
HG_DIM = 512
HG_HEADS = 4
HG_HEAD_DIM = HG_DIM // HG_HEADS
HG_CHUNK = 16
D_FF = 2816
CONV_K = 3
N_MOD = 6
EPS = 1e-6

OFF_FNET = 0
OFF_Q = OFF_FNET + FNET_DIM
OFF_FF = OFF_Q + HG_DIM
OFF_FB = OFF_FF + HG_DIM
OFF_I = OFF_FB + HG_DIM
OFF_G = OFF_I + HG_DIM
OFF_GA = OFF_G + HG_DIM
OFF_GB = OFF_GA + D_MODEL
IN_DIM = OFF_GB + D_MODEL

kernel_name = 'fnet_hgrn2_convffn_hybrid_dit'


def rmsnorm(x, g):
    xf = x.astype(jnp.float32)
    y = xf * lax.rsqrt(jnp.mean(xf * xf, axis=-1, keepdims=True) + EPS) * g.astype(jnp.float32)
    return y.astype(x.dtype)


def modulate(h, shift, scale):
    return h * (1.0 + scale) + shift


def fourier_mix(u):
    b, l, _ = u.shape
    ug = u.astype(jnp.float32).reshape(b, l, FNET_GROUPS, FNET_GROUP_DIM)
    y = jnp.fft.fft2(ug, axes=(1, 3), norm='ortho').real
    return y.reshape(b, l, FNET_DIM).astype(u.dtype)


def hgrn_chunk_scan(q, k, v, logf, s0):
    b, l, h, _ = q.shape
    dv = v.shape[-1]
    n = l // HG_CHUNK

    def to_chunks(t):
        return t.reshape(b, n, HG_CHUNK, h, t.shape[-1]).transpose(1, 0, 3, 2, 4)

    qc, kc, vc, gc = to_chunks(q), to_chunks(k), to_chunks(v), to_chunks(logf)
    lower = jnp.tril(jnp.ones((HG_CHUNK, HG_CHUNK), dtype=bool))[:, :, None]

    def step(state, inp):
        qi, ki, vi, gi = inp
        cum = jnp.cumsum(gi, axis=-2)
        total = cum[..., -1:, :]
        rel = cum[..., :, None, :] - cum[..., None, :, :]
        decay = jnp.exp(jnp.where(lower, rel, -jnp.inf))
        attn = jnp.einsum('bhtd,bhsd,bhtsd->bhts', qi, ki, decay)
        o = (jnp.einsum('bhts,bhsv->bhtv', attn, vi)
             + jnp.einsum('bhtd,bhdv->bhtv', qi * jnp.exp(cum), state))
        k_dec = ki * jnp.exp(total - cum)
        new_state = state * jnp.exp(total)[..., 0, :, None] + jnp.einsum('bhsd,bhsv->bhdv', k_dec, vi)
        return new_state, o

    s_final, o = lax.scan(step, s0, (qc, kc, vc, gc))
    o = o.transpose(1, 0, 3, 2, 4).reshape(b, l, h, dv)
    return o, s_final


def hgrn_inputs(u, lb):
    uf = u.astype(jnp.float32)
    b, l, _ = u.shape

    def heads(t):
        return t.reshape(b, l, HG_HEADS, HG_HEAD_DIM)

    q = heads(jax.nn.silu(uf[..., OFF_Q:OFF_Q + HG_DIM]))
    v = heads(uf[..., OFF_I:OFF_I + HG_DIM])
    f_fwd = lb + (1.0 - lb) * jax.nn.sigmoid(uf[..., OFF_FF:OFF_FF + HG_DIM])
    f_bwd = lb + (1.0 - lb) * jax.nn.sigmoid(uf[..., OFF_FB:OFF_FB + HG_DIM])
    return (q, v, heads(1.0 - f_fwd), heads(jnp.log(f_fwd)), heads(1.0 - f_bwd), heads(jnp.log(f_bwd)))


def hgrn_bidir(q, v, k_f, g_f, k_b, g_b, s0_f, s0_b):
    o_f, s_f = hgrn_chunk_scan(q, k_f, v, g_f, s0_f)
    flip = lambda t: jnp.flip(t, axis=1)
    o_b, s_b = hgrn_chunk_scan(flip(q), flip(k_b), flip(v), flip(g_b), s0_b)
    return o_f + flip(o_b), s_f, s_b


def hgrn_readout(u, o, onorm_g):
    b, l = o.shape[0], o.shape[1]
    on = o * lax.rsqrt(jnp.mean(o * o, axis=-1, keepdims=True) + EPS)
    on = on.reshape(b, l, HG_DIM) * onorm_g.astype(jnp.float32)
    on = on * jax.nn.silu(u[..., OFF_G:OFF_G + HG_DIM].astype(jnp.float32))
    return on.astype(u.dtype)


def merge_branches(u, y_hgrn, w_a, w_b, w_out):
    y_a = fourier_mix(u[..., OFF_FNET:OFF_FNET + FNET_DIM]) @ w_a
    y_b = y_hgrn @ w_b
    g_a = jax.nn.sigmoid(u[..., OFF_GA:OFF_GB])
    g_b = jax.nn.sigmoid(u[..., OFF_GB:IN_DIM])
    return (g_a * y_a + g_b * y_b) @ w_out


def conv_ffn(h, w_up, conv_w, conv_b, w_down, rows, cols):
    b, l, _ = h.shape
    z = (h @ w_up).reshape(b, rows, cols, 2 * D_FF)
    z = lax.conv_general_dilated(z, conv_w[:, :, None, :], window_strides=(1, 1), padding='SAME',
                                 dimension_numbers=('NHWC', 'HWIO', 'NHWC'),
                                 feature_group_count=2 * D_FF)
    z = z.reshape(b, l, 2 * D_FF) + conv_b
    return (jax.nn.silu(z[..., :D_FF]) * z[..., D_FF:]) @ w_down


def setup_inputs(seed: int = 0) -> dict:
    key = jax.random.key(seed)
    ks = jax.random.split(key, 19)

    def nrm(k, shape):
        return jax.random.normal(k, shape, jnp.float32)

    def w(k, shape, fan_in, scale=1.0):
        return nrm(k, shape) * (scale * fan_in ** -0.5)

    def gain(k, shape):
        return 1.0 + 0.1 * nrm(k, shape)

    return {
        'x': nrm(ks[0], (BATCH, SEQ, D_MODEL)),
        'c': nrm(ks[1], (BATCH, D_MODEL)),
        'ctx': nrm(ks[2], (BATCH, CTX_LEN, D_MODEL)),
        'c_ctx': nrm(ks[3], (D_MODEL,)),
        'ada_w': w(ks[4], (DEPTH, D_MODEL, N_MOD * D_MODEL), D_MODEL, 0.5),
        'ada_b': 0.02 * nrm(ks[5], (DEPTH, N_MOD * D_MODEL)),
        'norm1_g': gain(ks[6], (DEPTH, D_MODEL)),
        'w_in': w(ks[7], (DEPTH, D_MODEL, IN_DIM), D_MODEL),
        'hg_lb': 0.1 * nrm(ks[8], (DEPTH + 1, HG_DIM)),
        'hg_onorm_g': gain(ks[9], (DEPTH, HG_DIM)),
        'w_a': w(ks[10], (DEPTH, FNET_DIM, D_MODEL), FNET_DIM),
        'w_b': w(ks[11], (DEPTH, HG_DIM, D_MODEL), HG_DIM),
        'w_out': w(ks[12], (DEPTH, D_MODEL, D_MODEL), D_MODEL),
        'norm2_g': gain(ks[13], (DEPTH, D_MODEL)),
        'ffn_up': w(ks[14], (DEPTH, D_MODEL, 2 * D_FF), D_MODEL),
        'ffn_conv_w': w(ks[15], (DEPTH, CONV_K, CONV_K, 2 * D_FF), CONV_K * CONV_K),
        'ffn_conv_b': 0.02 * nrm(ks[16], (DEPTH, 2 * D_FF)),
        'ffn_down': w(ks[17], (DEPTH, D_FF, D_MODEL), D_FF),
        'final_g': gain(ks[18], (D_MODEL,)),
    }


def reference(x, c, ctx, c_ctx, ada_w, ada_b, norm1_g, w_in, hg_lb, hg_onorm_g, w_a, w_b, w_out,
              norm2_g, ffn_up, ffn_conv_w, ffn_conv_b, ffn_down, final_g):
    b, seq, _ = x.shape
    rows = seq // GRID_W
    ctx_len = ctx.shape[1]
    lb_all = jnp.cumsum(jax.nn.softmax(hg_lb.astype(jnp.float32), axis=0), axis=0)
    s_zero = jnp.zeros((b, HG_HEADS, HG_HEAD_DIM, HG_HEAD_DIM), jnp.float32)
    for layer in range(DEPTH):
        last = layer == DEPTH - 1
        mx = (jax.nn.silu(c) @ ada_w[layer] + ada_b[layer]).reshape(b, 1, N_MOD, D_MODEL)
        mc = (jax.nn.silu(c_ctx) @ ada_w[layer] + ada_b[layer]).reshape(N_MOD, D_MODEL)
        lb = lb_all[layer]

        hx = modulate(rmsnorm(x, norm1_g[layer]), mx[:, :, 0], mx[:, :, 1])
        hc = modulate(rmsnorm(ctx, norm1_g[layer]), mc[0], mc[1])
        ux = hx @ w_in[layer]
        uc = hc @ w_in[layer]
        oc, sc_f, sc_b = hgrn_bidir(*hgrn_inputs(uc, lb), s_zero, s_zero)
        ox, _, _ = hgrn_bidir(*hgrn_inputs(ux, lb), sc_f, sc_b)
        yx = merge_branches(ux, hgrn_readout(ux, ox, hg_onorm_g[layer]), w_a[layer], w_b[layer], w_out[layer])
        x = x + mx[:, :, 2] * yx
        if not last:
            yc = merge_branches(uc, hgrn_readout(uc, oc, hg_onorm_g[layer]), w_a[layer], w_b[layer], w_out[layer])
            ctx = ctx + mc[2] * yc

        h2 = modulate(rmsnorm(x, norm2_g[layer]), mx[:, :, 3], mx[:, :, 4])
        x = x + mx[:, :, 5] * conv_ffn(h2, ffn_up[layer], ffn_conv_w[layer], ffn_conv_b[layer],
                                       ffn_down[layer], rows, GRID_W)
        if not last:
            h2c = modulate(rmsnorm(ctx, norm2_g[layer]), mc[3], mc[4])
            ctx = ctx + mc[5] * conv_ffn(h2c, ffn_up[layer], ffn_conv_w[layer], ffn_conv_b[layer],
                                         ffn_down[layer], 1, ctx_len)
    return rmsnorm(x, final_g)
```

```python
import contextlib
import numpy as np
import ml_dtypes
import concourse.bass as bass
import concourse.mybir as mybir
from concourse.bass_utils import run_bass_kernel_spmd

F32 = mybir.dt.float32
BF16 = mybir.dt.bfloat16
AF = mybir.ActivationFunctionType
ALU = mybir.AluOpType

D = 1024
SEQ = 2048
CTX = 256
NT = SEQ // 128
NCH = NT + 2
DFF = 2816
NFF = DFF // 128
EPS = 1e-6

DEBUG = None


class Buf:
    __slots__ = ("name", "last_write", "readers", "excl")

    def __init__(self, name="", excl=False):
        self.name = name
        self.last_write = None
        self.readers = []
        self.excl = excl


class Reg:
    __slots__ = ("ap", "bufs")

    def __init__(self, ap, bufs):
        self.ap = ap
        self.bufs = list(bufs)

    def __getitem__(self, key):
        return Reg(self.ap[key], self.bufs)

    def v(self, fn):
        return Reg(fn(self.ap), self.bufs)


class _Op:
    __slots__ = ("eng", "fn", "deps", "seq", "signal", "is_dma", "dma_sem", "dma_val", "waits", "idx", "attach_ok")


class Prog:
    ENGS = ("pe", "act", "dve", "pool", "sp")

    def __init__(self, nc, n_dma_sems=48):
        self.nc = nc
        self.ops = []
        self.n_dma_sems = n_dma_sems
        self.eng_obj = {"pe": nc.tensor, "act": nc.scalar, "dve": nc.vector, "pool": nc.gpsimd, "sp": nc.sync}
        self.last_on = {e: None for e in self.ENGS}
        self.dma_since_barrier = []

    def op(self, eng, fn, reads=(), writes=(), dma=False, extra_deps=(), attach_ok=None):
        o = _Op()
        o.eng = eng
        o.attach_ok = (eng in ("dve", "pool")) if attach_ok is None else attach_ok
        o.fn = fn
        o.is_dma = dma
        o.idx = len(self.ops)
        deps = set()
        for b in writes:
            if b.last_write is not None:
                deps.add(b.last_write)
            for r in b.readers:
                deps.add(r)
        if not dma and eng == "pe":
            deps = {d for d in deps if self.ops[d].is_dma or self.ops[d].eng != eng}
        for b in reads:
            if b.last_write is not None:
                deps.add(b.last_write)
            if b.excl:
                for r in b.readers:
                    if self.ops[r].eng != eng:
                        deps.add(r)
        deps.update(extra_deps)
        o.deps = deps
        o.signal = False
        self.ops.append(o)
        for b in reads:
            b.readers.append(o.idx)
        for b in writes:
            b.last_write = o.idx
            b.readers = []
        if dma:
            self.dma_since_barrier.append(o.idx)
        elif fn is not None:
            self.last_on[eng] = o.idx
        return o.idx

    def barrier(self):
        deps = [v for v in self.last_on.values() if v is not None] + list(self.dma_since_barrier)
        self.dma_since_barrier = []
        for e in self.ENGS:
            self.op(e, None, extra_deps=deps)

    def emit(self, final_wait_eng="sp"):
        nc = self.nc
        ops = self.ops
        seqc = {e: 0 for e in self.ENGS}
        for o in ops:
            if not o.is_dma:
                seqc[o.eng] += 1
                o.seq = seqc[o.eng]
        dma_prev = [None] * self.n_dma_sems
        dma_cnt = [0] * self.n_dma_sems
        half = self.n_dma_sems // 2
        kq = {"sp": 0, "pool": 0}
        for o in ops:
            if o.is_dma:
                if o.eng == "sp":
                    j = kq["sp"] % half
                    kq["sp"] += 1
                else:
                    j = half + kq["pool"] % half
                    kq["pool"] += 1
                o.dma_sem = j
                dma_cnt[j] += 16
                o.dma_val = dma_cnt[j]
                if dma_prev[j] is not None:
                    o.deps.add(dma_prev[j])
                dma_prev[j] = o.idx
        know = {e: {x: 0 for x in self.ENGS} for e in self.ENGS}
        know_dma = {e: {} for e in self.ENGS}
        snap = {}
        for o in ops:
            E = o.eng
            waits = []
            comp = {}
            for d in o.deps:
                od = ops[d]
                if od.is_dma:
                    if know_dma[E].get(od.dma_sem, 0) < od.dma_val:
                        waits.append(("dma", od.dma_sem, od.dma_val))
                        know_dma[E][od.dma_sem] = od.dma_val
                else:
                    cur = comp.get(od.eng)
                    if cur is None or ops[cur].seq < od.seq:
                        comp[od.eng] = d
            for e2, d in sorted(comp.items(), key=lambda kv: -ops[kv[1]].seq):
                od = ops[d]
                if know[E][e2] >= od.seq:
                    continue
                od.signal = True
                waits.append(("eng", d))
                know[E][e2] = od.seq
                sn = snap[d]
                for x in self.ENGS:
                    if sn[x] > know[E][x]:
                        know[E][x] = sn[x]
            o.waits = waits
            if not o.is_dma:
                snap[o.idx] = dict(know[E])
        sigc = {e: 0 for e in self.ENGS}
        semval = {}
        for o in ops:
            if not o.is_dma and o.signal:
                sigc[o.eng] += 1
                semval[o.idx] = sigc[o.eng]
        self.stats = dict(n_ops=len(ops), n_sig=dict(sigc), n_waits=sum(len(o.waits) for o in ops))
        with contextlib.ExitStack() as st:
            esem = {e: st.enter_context(nc.semaphore("s_" + e)) for e in self.ENGS}
            dsem = [st.enter_context(nc.semaphore("d%d" % j)) for j in range(self.n_dma_sems)]
            for o in ops:
                eng = self.eng_obj[o.eng]
                waits = list(o.waits)
                attach = waits.pop() if (waits and o.fn is not None and (o.is_dma or o.attach_ok)) else None
                for w in waits:
                    if w[0] == "dma":
                        eng.wait_ge(dsem[w[1]], w[2])
                    else:
                        od = ops[w[1]]
                        eng.wait_ge(esem[od.eng], semval[od.idx])
                if o.fn is None:
                    assert not o.signal
                    continue
                inst = o.fn()
                if attach is not None:
                    if attach[0] == "dma":
                        inst._wait_ge(dsem[attach[1]], attach[2])
                    else:
                        od = ops[attach[1]]
                        inst._wait_ge(esem[od.eng], semval[od.idx])
                if o.is_dma:
                    inst.then_inc(dsem[o.dma_sem], 16)
                elif o.signal:
                    inst.then_inc(esem[o.eng], 1)
            eng = self.eng_obj[final_wait_eng]
            for j in range(self.n_dma_sems):
                if dma_cnt[j] > 0:
                    eng.wait_ge(dsem[j], dma_cnt[j])


class Arena:
    def __init__(self, nc, st, nbytes):
        self.nbytes = nbytes
        self.t = st.enter_context(nc.sbuf_tensor("arena", [128, nbytes // 4], F32))
        self.free = [(0, nbytes)]
        self.regs = {}
        self.peak = 0

    def alloc(self, name, nbytes, hi=False):
        nbytes = (nbytes + 63) // 64 * 64
        assert name not in self.regs, name
        if hi:
            for i in range(len(self.free) - 1, -1, -1):
                o, s = self.free[i]
                if s >= nbytes:
                    if s == nbytes:
                        self.free.pop(i)
                    else:
                        self.free[i] = (o, s - nbytes)
                    self.regs[name] = (o + s - nbytes, nbytes)
                    return o + s - nbytes
            raise MemoryError("arena full (hi) allocating %s (%d B); regs=%s" % (name, nbytes, sorted(self.regs.items(), key=lambda kv: kv[1])))
        for i, (o, s) in enumerate(self.free):
            if s >= nbytes:
                if s == nbytes:
                    self.free.pop(i)
                else:
                    self.free[i] = (o + nbytes, s - nbytes)
                self.regs[name] = (o, nbytes)
                self.peak = max(self.peak, o + nbytes)
                return o
        raise MemoryError("arena full allocating %s (%d B); regs=%s" % (name, nbytes, sorted(self.regs.items(), key=lambda kv: kv[1])))

    def release(self, *names):
        for name in names:
            o, s = self.regs.pop(name)
            self.free.append((o, s))
        self.free.sort()
        merged = []
        for o, s in self.free:
            if merged and merged[-1][0] + merged[-1][1] == o:
                merged[-1] = (merged[-1][0], merged[-1][1] + s)
            else:
                merged.append((o, s))
        self.free = merged

    def tile(self, name, dtype, shape, hi=False):
        esz = 4 if dtype == F32 else 2
        n = int(np.prod(shape[1:]))
        o = self.alloc(name, n * esz, hi)
        ap = self.t[:, o // 4: o // 4 + (n * esz + 3) // 4]
        if dtype != F32:
            ap = ap.bitcast(dtype)[:, 0:n]
        if len(shape) == 3:
            ap = ap.rearrange("p (a b) -> p a b", a=shape[1])
        elif len(shape) == 4:
            ap = ap.rearrange("p (a b c) -> p a b c", a=shape[1], b=shape[2])
        return Reg(ap, [Buf(name)])


def _bufs(*regs):
    out = []
    for r in regs:
        if isinstance(r, Reg):
            out.extend(r.bufs)
    return out


def sub(reg, ap_fn, name=None):
    return Reg(ap_fn(reg.ap), [Buf(name or "sub")])


def build(debug=None):
    nc = bass.Bass("TRN2", target_bir_lowering=False)

    def din(name, shape, dt=F32):
        return Reg(nc.dram_tensor(name, list(shape), dt, kind="ExternalInput").ap(), [])

    d_x = din("x", [SEQ, D])
    d_ctx = din("ctx", [CTX, D])
    d_ccol = din("ccol", [128, 16])
    d_adaw = din("ada_w", [D, 6 * D])
    d_adab = din("ada_b", [6 * D])
    d_n1g = din("norm1_g", [D])
    d_win = din("w_in", [D, 5120])
    d_hglb = din("hg_lb", [2, 512])
    d_og = din("og", [128, 4])
    d_ogrow = din("og_row", [512])
    d_wa = din("w_a", [512, D])
    d_wb = din("w_b", [512, D])
    d_wout = din("w_out", [D, D])
    d_n2g = din("norm2_g", [D])
    d_wup = din("ffn_up", [D, 2 * DFF])
    d_convw = din("convw", [128, 2 * NFF, 9])
    d_convb = din("convb", [128, 2 * NFF])
    d_wdown = din("ffn_down", [DFF, D])
    d_fg = din("final_g", [D])
    d_ident = din("ident", [128, 128], BF16)
    d_trimf = din("trimf", [128, 128])
    d_trimb = din("trimb", [128, 128])
    d_maskf = din("maskf", [128, 512], BF16)
    d_maskb = din("maskb", [128, 512], BF16)
    d_self = din("self", [128, 4])
    d_selb = din("selb", [128, 4])
    d_ones = din("ones_bf", [128, 128], BF16)
    d_cc = din("cc", [128, 128], BF16)
    d_scn = din("scn", [128, 128], BF16)
    d_dftc = din("dftc", [SEQ, SEQ // 2], BF16)
    d_dfts = din("dfts", [SEQ, SEQ // 2], BF16)
    d_scp = din("scp", [128, 128], BF16)
    d_out = Reg(nc.dram_tensor("out", [SEQ, D], F32, kind="ExternalOutput").ap(), [Buf("out")])
    dbg_out = {}

    st = contextlib.ExitStack()
    with st:
        class _Stop(Exception):
            pass

        hold = {}

        def body():
            AR = Arena(nc, st, 212000 // 64 * 64)
            banks = []
            for i in range(8):
                t = st.enter_context(nc.psum_tensor("bank%d" % i, [128, 512], F32))
                banks.append(Reg(t[:], [Buf("bank%d" % i, excl=True)]))
            bankbf = [Reg(b.ap.bitcast(BF16), b.bufs) for b in banks]
            P = Prog(nc)
            hold['P'] = P
            hold['AR'] = AR

            def A(x):
                return x.ap if isinstance(x, Reg) else x

            def ACT(out, in_, func, accum_out=None, **kw):
                kwap = {k: A(v) for k, v in kw.items()}
                if accum_out is not None:
                    kwap["accum_out"] = accum_out.ap
                P.op("act", lambda: nc.scalar.activation(out=out.ap, in_=in_.ap, func=func, **kwap),
                     reads=_bufs(in_, *kw.values()), writes=_bufs(out, accum_out), attach_ok=(accum_out is None))

            def TT(eng, out, in0, in1, op):
                e = nc.vector if eng == "dve" else nc.gpsimd
                P.op(eng, lambda: e.tensor_tensor(out=out.ap, in0=in0.ap, in1=in1.ap, op=op),
                     reads=_bufs(in0, in1), writes=_bufs(out))

            def STT(out, in0, scalar, in1, op0, op1):
                P.op("dve", lambda: nc.vector.scalar_tensor_tensor(out=out.ap, in0=in0.ap, scalar=A(scalar), in1=in1.ap, op0=op0, op1=op1),
                     reads=_bufs(in0, scalar, in1), writes=_bufs(out))

            def TS(eng, out, in0, s1, s2, op0, op1):
                e = nc.vector if eng == "dve" else nc.gpsimd
                P.op(eng, lambda: e.tensor_scalar(out=out.ap, in0=in0.ap, scalar1=A(s1), scalar2=A(s2), op0=op0, op1=op1),
                     reads=_bufs(in0, s1, s2), writes=_bufs(out))

            def CP(eng, out, in_):
                if eng == "act":
                    ACT(out, in_, AF.Copy)
                else:
                    e = nc.vector if eng == "dve" else nc.gpsimd
                    P.op(eng, lambda: e.tensor_copy(out=out.ap, in_=in_.ap), reads=_bufs(in_), writes=_bufs(out))

            def RECIP(out, in_):
                P.op("dve", lambda: nc.vector.reciprocal(out=out.ap, in_=in_.ap), reads=_bufs(in_), writes=_bufs(out))

            def MEMSET(eng, out, val):
                e = nc.vector if eng == "dve" else nc.gpsimd
                P.op(eng, lambda: e.memset(out.ap, val), writes=_bufs(out))

            def MM(out, lhsT, rhs, start, stop):
                P.op("pe", lambda: nc.tensor.matmul(out.ap, lhsT=lhsT.ap, rhs=rhs.ap, start=start, stop=stop),
                     reads=_bufs(lhsT, rhs), writes=_bufs(out))

            def TR(out, in_, ident):
                P.op("pe", lambda: nc.tensor.transpose(out=out.ap, in_=in_.ap, identity=ident.ap),
                     reads=_bufs(in_, ident), writes=_bufs(out))

            def DMA(q, out, in_):
                e = P.eng_obj[q]
                P.op(q, lambda: e.dma_start(out=out.ap, in_=in_.ap), reads=_bufs(in_), writes=_bufs(out), dma=True)

            def bcast_row(dreg):
                return dreg.v(lambda a: a.partition_broadcast(128))

            def wview(dreg, c0, c1):
                return dreg.v(lambda a: a[:, c0:c1].rearrange("(kc p) n -> p kc n", p=128))

            def dump(name, reg, shape, dt=F32):
                P.barrier()
                t = Reg(nc.dram_tensor("dbg_" + name, list(shape), dt, kind="ExternalOutput").ap(), [Buf("dbg")])
                DMA("sp", t, reg)
                dbg_out[name] = t

            ident = AR.tile("ident", BF16, [128, 128])
            trimf = AR.tile("trimf", F32, [128, 128])
            trimb = AR.tile("trimb", F32, [128, 128])
            maskf = AR.tile("maskf", BF16, [128, 512])
            maskb = AR.tile("maskb", BF16, [128, 512])
            self_ = AR.tile("self", F32, [128, 4])
            selb = AR.tile("selb", F32, [128, 4])
            ones_bf = AR.tile("ones_bf", BF16, [128, 128])
            cc = AR.tile("cc", BF16, [128, 128])
            scn = AR.tile("scn", BF16, [128, 128])
            scp = AR.tile("scp", BF16, [128, 128])
            og = AR.tile("og", F32, [128, 4])
            lb_bc = AR.tile("lb_bc", F32, [128, 512])
            lnoml_bc = AR.tile("lnoml_bc", F32, [128, 512])
            convw = AR.tile("convw", F32, [128, 2 * NFF, 9])
            convb = AR.tile("convb", F32, [128, 2 * NFF])
            rstd_all = AR.tile("rstd_all", F32, [128, NCH])
            ss = AR.tile("ss", F32, [128, 2])
            lnv = AR.tile("lnv", F32, [128, 2])
            for t, dsrc in ((ident, d_ident), (trimf, d_trimf), (trimb, d_trimb), (maskf, d_maskf), (maskb, d_maskb),
                            (self_, d_self), (selb, d_selb), (ones_bf, d_ones), (cc, d_cc), (scn, d_scn), (scp, d_scp), (og, d_og),
                            (convw, d_convw), (convb, d_convb)):
                DMA("sp", t, dsrc)

            A1 = AR.tile("A1", F32, [128, D], hi=True)
            B1 = AR.tile("B1", F32, [128, D], hi=True)
            A1c = AR.tile("A1c", F32, [128, D], hi=True)
            B1c = AR.tile("B1c", F32, [128, D], hi=True)

            t0 = AR.tile("p0_t0", F32, [128, 512])
            t1 = AR.tile("p0_t1", F32, [128, 512])
            DMA("sp", t0, bcast_row(d_hglb[0]))
            DMA("sp", t1, bcast_row(d_hglb[1]))
            TT("dve", t0, t0, t1, ALU.subtract)
            ACT(t1, t0, AF.Exp, scale=-1.0)
            TS("dve", t1, t1, 1.0, None, ALU.add, ALU.bypass)
            RECIP(lb_bc, t1)
            ACT(lnoml_bc, lb_bc, AF.Ln, scale=-1.0, bias=1.0)

            ccol = AR.tile("ccol", F32, [128, 16], hi=True)
            csil = AR.tile("csil", F32, [128, 16], hi=True)
            cb_bf = AR.tile("cb_bf", BF16, [128, 16, 128], hi=True)
            DMA("sp", ccol, d_ccol)
            ACT(csil, ccol, AF.Exp, scale=-1.0)
            TS("dve", csil, csil, 1.0, None, ALU.add, ALU.bypass)
            RECIP(csil, csil)
            TT("dve", csil, csil, ccol, ALU.mult)
            CP("dve", cb_bf, csil.v(lambda a: a.unsqueeze(2).to_broadcast([128, 16, 128])))

            wslots = [AR.tile("wslot%d" % i, BF16, [128, 8, 512]) for i in range(2)]
            adab = AR.tile("adab", F32, [128, D])
            gbc = AR.tile("gbc", F32, [128, D])
            mtmp = AR.tile("mtmp", F32, [128, 512])
            wk = [0]

            def mods(j, handler, fixed_bank=None):
                DMA("sp", adab, bcast_row(d_adab[j * D:(j + 1) * D]))
                for hh in range(2):
                    slot = wslots[wk[0] % 2]
                    wk[0] += 1
                    c0 = j * D + hh * 512
                    DMA("pool", slot, wview(d_adaw, c0, c0 + 512))
                    for which in range(2):
                        if not handler(which, None, None, None):
                            continue
                        bk = banks[(wk[0] * 2 + which) % 8] if fixed_bank is None else banks[fixed_bank]
                        for kc in range(8):
                            MM(bk, cb_bf[:, which * 8 + kc, :], slot[:, kc, :], kc == 0, kc == 7)
                        handler(which, hh, bk, adab[:, hh * 512:(hh + 1) * 512])

            DMA("sp", gbc, bcast_row(d_n1g))

            def h_shift(dst_lat, dst_ctx):
                def h(which, hh, bk, ab):
                    dst = dst_lat if which == 0 else dst_ctx
                    if dst is None:
                        return False
                    if hh is None:
                        return True
                    TT("dve", dst[:, hh * 512:(hh + 1) * 512], bk, ab, ALU.add)
                    return True
                return h

            def h_scale(dst_lat, dst_ctx):
                def h(which, hh, bk, ab):
                    dst = dst_lat if which == 0 else dst_ctx
                    if dst is None:
                        return False
                    if hh is None:
                        return True
                    TT("dve", mtmp, bk, ab, ALU.add)
                    STT(dst[:, hh * 512:(hh + 1) * 512], mtmp, 1.0, gbc[:, hh * 512:(hh + 1) * 512], ALU.add, ALU.mult)
                    return True
                return h

            mods(0, h_shift(B1, B1c))
            mods(1, h_scale(A1, A1c))

            if debug == "mods":
                dump("A1", A1, [128, D]); dump("B1", B1, [128, D]); dump("A1c", A1c, [128, D]); dump("B1c", B1c, [128, D])
                dump("lb", lb_bc, [128, 512])
                raise _Stop()

            Wh = [AR.tile("Wh%d" % i, BF16, [128, 8, 512]) for i in range(5)]
            for i in (1, 2, 3):
                DMA("pool", Wh[i], wview(d_win, 512 + i * 512, 1024 + i * 512))
            late = P.dma_since_barrier[-2:]
            P.dma_since_barrier = P.dma_since_barrier[:-2]
            P.barrier()
            P.dma_since_barrier.extend(late)
            for i in (0, 4):
                DMA("pool", Wh[i], wview(d_win, 512 + i * 512, 1024 + i * 512))
            AR.release("wslot0", "wslot1", "adab", "gbc", "mtmp", "p0_t0", "p0_t1")
            sggT = [None] * NT
            QbT = [None] * NT
            opart = [None] * NT
            sgg_all = AR.tile("sggT_all", BF16, [128, NT, 4, 128], hi=True)
            QbT_all = AR.tile("QbT_all", BF16, [128, NT, 4, 128], hi=True)
            opart_all = AR.tile("opart_all", BF16, [128, NT, 4, 128], hi=True)
            bPb_all = AR.tile("bPb_all", BF16, [128, NCH, 4, 128], hi=True)
            abcb_all = AR.tile("abcb_all", F32, [128, NCH, 4, 4], hi=True)
            for n in range(NT):
                sggT[n] = sub(sgg_all, lambda a, n=n: a[:, n], "sgg%d" % n)
                QbT[n] = sub(QbT_all, lambda a, n=n: a[:, n], "QbT%d" % n)
                opart[n] = sub(opart_all, lambda a, n=n: a[:, n], "opart%d" % n)
            bPb = [sub(bPb_all, lambda a, c=c: a[:, c], "bPb%d" % c) for c in range(NCH)]
            abcb = [sub(abcb_all, lambda a, c=c: a[:, c], "abcb%d" % c) for c in range(NCH)]
            rstd_c = [sub(rstd_all, lambda a, c=c: a[:, c:c + 1], "rstd%d" % c) for c in range(NCH)]

            xts = [AR.tile("xt%d" % i, F32, [128, D]) for i in range(2)]
            junk = AR.tile("junk", BF16, [128, D])
            hx = AR.tile("hx", BF16, [128, D])
            hxT = [AR.tile("hxT%d" % i, BF16, [128, 8, 128]) for i in range(2)]
            Ta = AR.tile("Ta", F32, [128, 512])
            q32 = AR.tile("q32", F32, [128, 512])
            Tb = AR.tile("Tb", F32, [128, 512])
            Vbf = AR.tile("Vbf", BF16, [128, 512])
            Tz = [[AR.tile("Tz%d%d" % (z, i), F32, [128, 512]) for i in range(3)] for z in range(2)]
            Qz = [AR.tile("Qz%d" % z, BF16, [128, 512]) for z in range(2)]
            Ktz = [AR.tile("Ktz%d" % z, BF16, [128, 512]) for z in range(2)]
            QfT = AR.tile("QfT", BF16, [128, 4, 128])
            KTz = [AR.tile("KTz%d" % z, BF16, [128, 4, 128]) for z in range(2)]
            Amz = [AR.tile("Am%d" % z, BF16, [128, 512]) for z in range(2)]
            Spf = AR.tile("Spf", BF16, [128, 4, 128])
            tmpS = AR.tile("tmpS", F32, [128, 4, 128])
            Sf = AR.tile("Sf", F32, [128, 4, 128])
            abcf = [AR.tile("abcf%d" % i, F32, [128, 4, 4]) for i in range(2)]
            MEMSET("pool", Sf, 0.0)
            og_bc = og.v(lambda a: a.unsqueeze(2).to_broadcast([128, 4, 128]))
            og_row = AR.tile("og_row", F32, [128, 512])
            DMA("sp", og_row, bcast_row(d_ogrow))
            sggtm = AR.tile("sggtm", BF16, [128, 512])
            trim = (trimf, trimb)
            sel = (self_, selb)
            mask = (maskf, maskb)
            K0, K1, K2, K3, K4, K5, K6, K7 = banks

            def norm_hx(ci, xt, Areg, Breg, have_rstd, hxT_out, tb=0):
                if not have_rstd:
                    ACT(junk, xt, AF.Square, accum_out=ss[:, 0:1])
                    ACT(lnv[:, 0:1], ss[:, 0:1], AF.Ln, scale=1.0 / D, bias=EPS)
                    ACT(rstd_c[ci], lnv[:, 0:1], AF.Exp, scale=-0.5)
                STT(xt, xt, rstd_c[ci], Areg, ALU.mult, ALU.mult)
                TT("dve", hx, xt, Breg, ALU.add)
                for kc in range(8):
                    TR(bankbf[tb][:, kc * 128:(kc + 1) * 128], hx[:, kc * 128:(kc + 1) * 128], ident)
                CP("dve", hxT_out, bankbf[tb].v(lambda a: a.rearrange("p (a b) -> p a b", a=8)))

            WZ = (K6, K0)

            def stageA(ci, part=None):
                lat = ci >= 2
                n = ci - 2
                xt = xts[ci % 2]
                hT = hxT[ci % 2]
                if part in (None, 0):
                    src = d_x[n * 128:(n + 1) * 128, :] if lat else d_ctx[ci * 128:(ci + 1) * 128, :]
                    DMA("sp", xt, src)
                    ACT(junk, xt, AF.Square, accum_out=ss[:, 0:1])
                    ACT(lnv[:, 0:1], ss[:, 0:1], AF.Ln, scale=1.0 / D, bias=EPS)
                    ACT(rstd_c[ci], lnv[:, 0:1], AF.Exp, scale=-0.5)
                    STT(xt, xt, rstd_c[ci], A1 if lat else A1c, ALU.mult, ALU.mult)
                    TT("dve", hx, xt, B1 if lat else B1c, ALU.add)
                if part in (None, 1):
                    for kc in range(8):
                        TR(bankbf[7][:, kc * 128:(kc + 1) * 128], hx[:, kc * 128:(kc + 1) * 128], ident)
                    CP("act", hT, bankbf[7].v(lambda a: a.rearrange("p (a b) -> p a b", a=8)))

            def stageA2(ci, part):
                lat = ci >= 2
                hT = hxT[ci % 2]
                if part == 0:
                    blks = [(1, K2), (2, K3)]
                elif part == 1:
                    blks = [(3, K4)] + ([(0, K1)] if lat else [])
                else:
                    blks = []
                for bi, bk in blks:
                    for kc in range(8):
                        MM(bk, hT[:, kc, :], Wh[bi][:, kc, :], kc == 0, kc == 7)
                if lat and part == 2:
                    for kc in range(8):
                        MM(K5, hT[:, kc, :], Wh[4][:, kc, :], kc == 0, kc == 7)

            T1 = [Tz[0][0], Tz[1][0]]
            T2 = [Tz[0][1], Tz[1][1]]
            T3 = [Tz[0][2], Tz[1][2]]
            Tg = AR.tile("Tg", F32, [128, 512])
            Qz2 = [Qz, [AR.tile("Qzb%d" % z, BF16, [128, 512]) for z in range(2)]]
            Ktz2 = [Ktz, [AR.tile("Ktzb%d" % z, BF16, [128, 512]) for z in range(2)]]
            Vbf2 = [Vbf, AR.tile("Vbfb", BF16, [128, 512])]
            Spf2 = [Spf, AR.tile("Spfb", BF16, [128, 4, 128])]

            def stageBfront(ci):
                lat = ci >= 2
                KM = (K2, K3)
                for z in range(2):
                    ACT(T1[z], KM[z], AF.Exp)
                CP("act", Vbf2[ci % 2], K4)
                if lat:
                    CP("dve", q32, K1)
                    CP("act", Tg, K5)

            def stageB1(ci):
                lat = ci >= 2
                n = ci - 2
                for z in range(2):
                    ACT(T2[z], T1[z], AF.Ln, bias=1.0)
                for z in range(2):
                    TT("dve", T1[z], T1[z], lb_bc, ALU.add)
                for z in range(2):
                    ACT(T1[z], T1[z], AF.Ln)
                for z in range(2):
                    TT("dve", T1[z], T1[z], T2[z], ALU.subtract)
                for z in range(2):
                    MM(WZ[z], trim[z], T1[z], True, True)
                for z in range(2):
                    for h in range(4):
                        MM(K7[:, z * 16 + h * 4: z * 16 + h * 4 + 4], T1[z][:, h * 128:(h + 1) * 128], sel[z], True, True)
                for z in range(2):
                    TT("dve", T2[z], lnoml_bc, T2[z], ALU.subtract)
                if lat:
                    ACT(Ta, q32, AF.Exp, scale=-1.0)
                    ACT(Tb, Tg, AF.Exp, scale=-1.0)
                    ACT(Ta, Ta, AF.Ln, bias=1.0)
                    ACT(Tb, Tb, AF.Ln, bias=1.0)
                    ACT(Ta, Ta, AF.Exp, scale=-1.0)
                    ACT(Tb, Tb, AF.Exp, scale=-1.0)
                    TT("dve", q32, q32, Ta, ALU.mult)
                    TT("dve", Tb, Tg, Tb, ALU.mult)
                    TT("dve", sggtm, Tb, og_row, ALU.mult)

            def stageGT(ci):
                n = ci - 2
                if n < 0:
                    return
                for h in range(4):
                    TR(bankbf[5][:, h * 128:(h + 1) * 128], sggtm[:, h * 128:(h + 1) * 128], ident)
                CP("act", sggT[n], bankbf[5][:, 0:512].v(lambda a: a.rearrange("p (h t) -> p h t", h=4)))

            def stageB2(ci):
                lat = ci >= 2
                n = ci - 2
                abc = [abcf[ci % 2], abcb[ci]]
                Qc, Kc, Vc = Qz2[ci % 2], Ktz2[ci % 2], Vbf2[ci % 2]
                for z in range(2):
                    ACT(abc[z].v(lambda a: a.rearrange("p h c -> p (h c)")), K7[:, z * 16: z * 16 + 16], AF.Exp)
                for z in range(2):
                    if lat:
                        ACT(T3[z], WZ[z], AF.Exp)
                    TT("dve", T2[z], T2[z], WZ[z], ALU.subtract)
                for z in range(2):
                    ACT(Kc[z], T2[z], AF.Exp)
                    if lat:
                        TT("dve", Qc[z], q32, T3[z], ALU.mult)
                for z in range(2):
                    for h in range(4):
                        MM(WZ[z][:, h * 128:(h + 1) * 128], Kc[z][:, h * 128:(h + 1) * 128], Vc[:, h * 128:(h + 1) * 128], True, True)
                if lat:
                    TT("dve", Spf2[ci % 2], Sf, abcf[ci % 2].v(lambda a: a[:, :, 1:2].to_broadcast([128, 4, 128])), ALU.mult)
                bf_bc = abcf[ci % 2].v(lambda a: a[:, :, 2:3].to_broadcast([128, 4, 128]))
                bb_bc = abcb[ci].v(lambda a: a[:, :, 2:3].to_broadcast([128, 4, 128]))
                TT("dve", tmpS, K6.v(lambda a: a.rearrange("p (h t) -> p h t", h=4)), bf_bc, ALU.mult)
                for h in range(4):
                    STT(Sf[:, h, :], Sf[:, h, :], abcf[ci % 2][:, h, 0:1], tmpS[:, h, :], ALU.mult, ALU.add)
                TT("dve", bPb[ci], K0.v(lambda a: a.rearrange("p (h t) -> p h t", h=4)), bb_bc, ALU.mult)

            KA = (K4, K1)

            def stageB3(ci, part):
                n = ci - 2
                if n < 0:
                    return
                Qc, Kc, Vc, Sp = Qz2[ci % 2], Ktz2[ci % 2], Vbf2[ci % 2], Spf2[ci % 2]
                tbank = (bankbf[7], bankbf[6])
                QT = (QfT, QbT[n])
                if part == 0:
                    for z in range(2):
                        for h in range(4):
                            TR(tbank[z][:, h * 128:(h + 1) * 128], Qc[z][:, h * 128:(h + 1) * 128], ident)
                        for h in range(4):
                            TR(tbank[z][:, 512 + h * 128: 512 + (h + 1) * 128], Kc[z][:, h * 128:(h + 1) * 128], ident)
                    for z in range(2):
                        CP("dve", QT[z], tbank[z][:, 0:512].v(lambda a: a.rearrange("p (h t) -> p h t", h=4)))
                        CP("act", KTz[z], tbank[z][:, 512:1024].v(lambda a: a.rearrange("p (h t) -> p h t", h=4)))
                elif part == 1:
                    for z in range(2):
                        for h in range(4):
                            MM(KA[z][:, h * 128:(h + 1) * 128], KTz[z][:, h, :], QT[z][:, h, :], True, True)
                    for z in range(2):
                        TT("dve", Amz[z], KA[z], mask[z], ALU.mult)
                else:
                    for h in range(4):
                        o_h = K5[:, h * 128:(h + 1) * 128]
                        MM(o_h, Vc[:, h * 128:(h + 1) * 128], Amz[0][:, h * 128:(h + 1) * 128], True, False)
                        MM(o_h, Vc[:, h * 128:(h + 1) * 128], Amz[1][:, h * 128:(h + 1) * 128], False, False)
                        MM(o_h, Sp[:, h, :], QfT[:, h, :], False, True)
                    CP("act", opart[n], K5.v(lambda a: a.rearrange("p (h t) -> p h t", h=4)))

            stageA(0)
            for part in range(3):
                stageA2(0, part)
            stageA(1)
            for ci in range(NCH):
                nxt = ci + 1 < NCH
                if ci + 2 < NCH:
                    stageA(ci + 2, 0)
                stageBfront(ci)
                stageB3(ci - 1, 0)
                if nxt:
                    stageA2(ci + 1, 0)
                stageB3(ci - 1, 1)
                stageB1(ci)
                stageB3(ci - 1, 2)
                if nxt:
                    stageA2(ci + 1, 1)
                stageB2(ci)
                stageGT(ci)
                if nxt:
                    stageA2(ci + 1, 2)
                if ci + 2 < NCH:
                    stageA(ci + 2, 1)
            for part in range(3):
                stageB3(NCH - 1, part)

            if debug == "p1":
                dump("Sf", Sf, [128, 4, 128])
                dump("opart", opart_all, [128, NT, 4, 128], BF16)
                dump("sgg", sgg_all, [128, NT, 4, 128], BF16)
                dump("abcb", abcb_all, [128, NCH, 4, 4])
                dump("bPb", bPb_all, [128, NCH, 4, 128], BF16)
                dump("rstd", rstd_all, [128, NCH])
                raise _Stop()

            P.barrier()
            AR.release("xt0", "xt1", "junk", "hx", "hxT0", "hxT1", "Ta", "q32", "Tb", "Vbf", "Tz00", "Tz01", "Tz02", "Tz10", "Tz11", "Tz12",
                       "Qz0", "Qz1", "Ktz0", "Ktz1", "QfT", "Tg", "og_row", "sggtm", "Qzb0", "Qzb1", "Ktzb0", "Ktzb1", "Vbfb", "Spfb", "KTz0", "KTz1", "Am0", "Am1", "Spf", "tmpS", "Sf", "abcf0", "abcf1",
                       "Wh0", "Wh1", "Wh2", "Wh3", "Wh4", "A1c", "B1c")

            yhT_all = AR.tile("yhT_all", BF16, [128, 4, SEQ], hi=True)
            xts = [AR.tile("xt%d" % i, F32, [128, D]) for i in range(2)]
            hx = AR.tile("hx", BF16, [128, D])
            mx2 = AR.tile("mx2", F32, [128, D])
            A2 = AR.tile("A2", F32, [128, D])
            B2 = AR.tile("B2", F32, [128, D])
            mx5 = AR.tile("mx5", F32, [128, D])
            yhT = [sub(yhT_all, lambda a, n=n: a[:, :, n * 128:(n + 1) * 128], "yhT%d" % n) for n in range(NT)]
            Sb = AR.tile("Sb", F32, [128, 4, 128])
            Spb_all = AR.tile("Spb_all", BF16, [128, NT, 4, 128])
            Spb = [sub(Spb_all, lambda a, n=n: a[:, n], "Spb%d" % n) for n in range(NT)]
            ot = [AR.tile("ot%d" % i, F32, [128, 512]) for i in range(2)]
            sq = [AR.tile("sq%d" % i, BF16, [128, 512]) for i in range(2)]
            rr = [AR.tile("rr%d" % i, F32, [128, 512]) for i in range(2)]
            Sb_h = [sub(Sb, lambda a, h=h: a[:, h, :], "Sb_h%d" % h) for h in range(4)]
            for h in range(4):
                MEMSET("pool", Sb_h[h], 0.0)
            Wf = AR.tile("Wf", BF16, [128, 8, 512])
            DMA("pool", Wf, wview(d_win, 0, 512))
            U_all = AR.tile("U_all", BF16, [128, NT, 512])
            U = [sub(U_all, lambda a, n=n: a[:, n], "U%d" % n) for n in range(NT)]
            hxT = [AR.tile("hxT%d" % i, BF16, [128, 8, 128], hi=True) for i in range(2)]

            def sweep_tile(n):
                xt = xts[n % 2]
                DMA("sp", xt, d_x[n * 128:(n + 1) * 128, :])
                norm_hx(n + 2, xt, A1, B1, True, hxT[n % 2], tb=4)
                bk = banks[5 + n % 2]
                for kc in range(8):
                    MM(bk, hxT[n % 2][:, kc, :], Wf[:, kc, :], kc == 0, kc == 7)
                CP("act", U[n], bk)

            def sb_update(ci):
                for h in range(4):
                    STT(Sb_h[h], Sb_h[h], abcb[ci][:, h, 0:1], bPb[ci][:, h, :], ALU.mult, ALU.add)

            sb_update(1)
            sb_update(0)
            if debug == "p2":
                dsb = AR.tile("dbg_sb", F32, [128, 4, 128])
                for h in range(4):
                    CP("dve", dsb[:, h, :], Sb_h[h])
                dump("Sb_ctx", dsb, [128, 4, 128])
            ot4 = ot + [AR.tile("ot%d" % i, F32, [128, 512]) for i in (2, 3)]
            hxs = [hx, AR.tile("hxs", BF16, [128, D])]

            def S1(k):
                xt = xts[k % 2]
                DMA("sp", xt, d_x[k * 128:(k + 1) * 128, :])
                STT(xt, xt, rstd_c[k + 2], A1, ALU.mult, ALU.mult)
                TT("dve", hxs[k % 2], xt, B1, ALU.add)

            def S2(k):
                for kc in range(8):
                    TR(bankbf[4][:, kc * 128:(kc + 1) * 128], hxs[k % 2][:, kc * 128:(kc + 1) * 128], ident)
                CP("act", hxT[k % 2], bankbf[4].v(lambda a: a.rearrange("p (a b) -> p a b", a=8)))

            def S3(k):
                bk = banks[5 + k % 2]
                for kc in range(8):
                    MM(bk, hxT[k % 2][:, kc, :], Wf[:, kc, :], kc == 0, kc == 7)
                CP("act", U[k], bk)

            def chain_step(i):
                n = NT - 1 - i
                ci = n + 2
                for h in range(4):
                    TS("dve", Spb[n][:, h, :], Sb_h[h], abcb[ci][:, h, 1:2], None, ALU.mult, ALU.bypass)
                sb_update(ci)

            chain_step(0)
            chain_step(1)

            def R1a(c):
                ko = banks[0 + 2 * (c % 2)]
                for h in range(4):
                    MM(ko[:, h * 128:(h + 1) * 128], Spb[c][:, h, :], QbT[c][:, h, :], True, True)
                TT("dve", ot4[c % 4], ko, opart[c].v(lambda a: a.rearrange("p h t -> p (h t)")), ALU.add)

            def R1b(c):
                km = banks[1 + 2 * (c % 2)]
                ACT(sq[c % 2], ot4[c % 4], AF.Square)
                MM(km, ones_bf, sq[c % 2], True, True)

            def R2a(c):
                km = banks[1 + 2 * (c % 2)]
                ACT(rr[c % 2], km, AF.Ln, scale=1.0 / 128, bias=EPS)
                ACT(rr[c % 2], rr[c % 2], AF.Exp, scale=-0.5)

            def R2b(c):
                TT("dve", ot4[c % 4], ot4[c % 4], rr[c % 2], ALU.mult)
                TT("dve", yhT[c], ot4[c % 4].v(lambda a: a.rearrange("p (h t) -> p h t", h=4)), sggT[c], ALU.mult)

            for i in range(NT + 3):
                cs = [NT - 1 - (i - d) for d in range(4)]
                if i + 2 < NT:
                    chain_step(i + 2)
                if i < NT:
                    S1(i)
                if 0 <= i - 1 < NT:
                    S2(i - 1)
                if 0 <= i - 2 < NT:
                    S3(i - 2)
                if 0 <= cs[0] < NT and i < NT:
                    R1a(cs[0])
                if 0 <= cs[1] < NT and 0 <= i - 1 < NT:
                    R1b(cs[1])
                if 0 <= cs[2] < NT and 0 <= i - 2 < NT:
                    R2a(cs[2])
                if 0 <= cs[3] < NT and 0 <= i - 3 < NT:
                    R2b(cs[3])

            if debug == "p2":
                dump("yhT", yhT_all, [128, 4, SEQ], BF16)
                raise _Stop()

            P.barrier()
            AR.release("ot2", "ot3", "hxs")
            AR.release("Sb", "Spb_all", "ot0", "ot1", "sq0", "sq1", "rr0", "rr1",
                       "sggT_all", "QbT_all", "opart_all", "bPb_all", "abcb_all")

            yfT_all = AR.tile("yfT_all", BF16, [128, 4, SEQ], hi=True)
            dslot = [AR.tile("dslot%d" % i, BF16, [128, NT, 512]) for i in range(2)]
            wslots = [AR.tile("wslot%d" % i, BF16, [128, 8, 512]) for i in range(2)]
            adab = AR.tile("adab", F32, [128, D])
            gbc = AR.tile("gbc", F32, [128, D])
            mtmp = AR.tile("mtmp", F32, [128, 512])
            DMA("sp", gbc, bcast_row(d_n2g))
            late_mods = [lambda: mods(2, h_shift(mx2, None), 7), lambda: mods(3, h_shift(B2, None), 7),
                         lambda: mods(4, h_scale(A2, None), 7), lambda: mods(5, h_shift(mx5, None), 7)]
            Wa = AR.tile("Wa", BF16, [128, 4, D])
            Wb = AR.tile("Wb", BF16, [128, 4, D])
            Pcs = [[AR.tile("Pcs%d%d" % (cs, g), BF16, [128, 512]) for g in range(4)] for cs in range(2)]
            dftv = [d.v(lambda a: a.rearrange("(lc p) k -> p lc k", p=128)) for d in (d_dftc, d_dfts)]
            Pc0 = AR.tile("Pc0", BF16, [128, 4, 2])
            for sl in range(2):
                for cs in range(2):
                    DMA("sp", dslot[cs], dftv[cs][:, :, sl * 512:(sl + 1) * 512])
                for cs in range(2):
                    if cs == 1:
                        late_mods[2 * sl]()
                        late_mods[2 * sl + 1]()
                        if sl == 0:
                            DMA("pool", Wa, d_wa.v(lambda a: a.rearrange("(g p) n -> p g n", p=128)))
                            DMA("pool", Wb, d_wb.v(lambda a: a.rearrange("(g p) n -> p g n", p=128)))
                    for g in range(4):
                        bk = banks[cs * 4 + g]
                        for lc in range(NT):
                            MM(bk, U[lc][:, g * 128:(g + 1) * 128], dslot[cs][:, lc, :], lc == 0, lc == NT - 1)
                        CP("act" if g % 2 == 0 else "dve", Pcs[cs][g], bk)
                for g in range(4):
                    bk = banks[g]
                    MM(bk, cc, Pcs[0][g], True, False)
                    MM(bk, scn, Pcs[1][g], False, True)
                    c0 = 1 + 512 * sl
                    CP("act" if g % 2 == 0 else "dve", sub(yfT_all, lambda a, g=g, c0=c0: a[:, g, c0:c0 + 512]), bk)
                    bm = banks[4 + g]
                    MM(bm, cc, Pcs[0][g], True, False)
                    MM(bm, scp, Pcs[1][g], False, True)
                    m0 = 1536 - 512 * sl
                    nm = 512 - sl
                    CP("dve" if g % 2 == 0 else "act",
                       sub(yfT_all, lambda a, g=g, m0=m0, nm=nm: a[:, g, m0 + 512 - nm:m0 + 512][:, ::-1]), bm[:, 0:nm])
            for g in range(4):
                for lc in range(NT):
                    MM(banks[g][:, 0:2], U[lc][:, g * 128:(g + 1) * 128], ones_bf[:, 0:2], lc == 0, lc == NT - 1)
                CP("dve", Pc0[:, g, :], banks[g][:, 0:2])
                MM(banks[4 + g][:, 0:2], cc, Pc0[:, g, :], True, True)
                CP("act", sub(yfT_all, lambda a, g=g: a[:, g, 0:1]), banks[4 + g][:, 0:1])
            P.barrier()
            if debug == "p2b":
                dump("yfT", yfT_all, [128, 4, SEQ], BF16)
                dump("U", U_all, [128, NT, 512], BF16)
                raise _Stop()
            AR.release("U_all", "dslot0", "dslot1", "Wf", "Pc0", *["Pcs%d%d" % (cs, g) for cs in range(2) for g in range(4)])
            AR.release("wslot0", "wslot1", "adab", "gbc", "mtmp", "ccol", "csil", "cb_bf")
            Wg = [AR.tile("Wg%d" % i, BF16, [128, 8, 512]) for i in range(4)]
            for i in (0, 2, 1, 3):
                DMA("pool", Wg[i], wview(d_win, 3072 + i * 512, 3072 + (i + 1) * 512))

            mT_all = AR.tile("mT_all", BF16, [128, 8, SEQ])
            hxTb = [AR.tile("hxTb%d" % i, BF16, [128, 8, 512]) for i in range(2)]
            sga = AR.tile("sga", F32, [128, 512])
            sgb = AR.tile("sgb", F32, [128, 512])
            m1 = AR.tile("m1", F32, [128, 512])
            m2 = AR.tile("m2", F32, [128, 512])
            mT = {}
            def prep_norm(tb, i):
                n = tb * 4 + i
                xt = xts[n % 2]
                DMA("sp", xt, d_x[n * 128:(n + 1) * 128, :])
                STT(xt, xt, rstd_c[n + 2], A1, ALU.mult, ALU.mult)
                TT("dve", hx, xt, B1, ALU.add)

            def prep_tr(tb, i):
                hb_ = hxTb[tb % 2]
                for kc in range(8):
                    TR(bankbf[0][:, kc * 128:(kc + 1) * 128], hx[:, kc * 128:(kc + 1) * 128], ident)
                CP("dve", hb_[:, :, i * 128:(i + 1) * 128], bankbf[0].v(lambda a: a.rearrange("p (a b) -> p a b", a=8)))

            for i in range(4):
                prep_norm(0, i)
                prep_tr(0, i)
            for tb in range(4):
                hb = hxTb[tb % 2]
                for dc in range(8):
                    if tb + 1 < 4 and dc % 2 == 0:
                        prep_norm(tb + 1, dc // 2)
                    s = 4 * (dc % 2)
                    kga, kgb, kya, kyb = banks[s], banks[s + 1], banks[s + 2], banks[s + 3]
                    ca = dc * 128
                    for kc in range(8):
                        MM(kga, Wg[ca // 512][:, kc, ca % 512: ca % 512 + 128], hb[:, kc, :], kc == 0, kc == 7)
                    cbk = 1024 + dc * 128
                    for kc in range(8):
                        MM(kgb, Wg[cbk // 512][:, kc, cbk % 512: cbk % 512 + 128], hb[:, kc, :], kc == 0, kc == 7)
                    for g in range(4):
                        MM(kya, Wa[:, g, dc * 128:(dc + 1) * 128], yfT_all[:, g, tb * 512:(tb + 1) * 512], g == 0, g == 3)
                    for h in range(4):
                        MM(kyb, Wb[:, h, dc * 128:(dc + 1) * 128], yhT_all[:, h, tb * 512:(tb + 1) * 512], h == 0, h == 3)
                    ACT(sga, kga, AF.Sigmoid)
                    ACT(sgb, kgb, AF.Sigmoid)
                    TT("dve", m1, sga, kya, ALU.mult)
                    TT("dve", m2, sgb, kyb, ALU.mult)
                    mT[(dc, tb)] = sub(mT_all, lambda a, dc=dc, tb=tb: a[:, dc, tb * 512:(tb + 1) * 512], "mT%d_%d" % (dc, tb))
                    TT("dve", mT[(dc, tb)], m1, m2, ALU.add)
                    if tb + 1 < 4 and dc % 2 == 1:
                        prep_tr(tb + 1, dc // 2)
            if debug == "p3a":
                P.barrier()
                dump("mT", mT_all, [128, 8, SEQ], BF16)
                raise _Stop()
            P.barrier()
            AR.release("Wg0", "Wg1", "Wg2", "Wg3", "Wa", "Wb", "hxTb0", "hxTb1", "sga", "sgb", "m1", "m2",
                       "yhT_all", "yfT_all", "A1", "B1", "hxT0", "hxT1")

            x1_all = AR.tile("x1_all", F32, [128, NT, D], hi=True)
            h2T_all = AR.tile("h2T_all", BF16, [128, 8, SEQ], hi=True)
            Wo = [AR.tile("Wo%d" % i, BF16, [128, 8, 512]) for i in range(2)]
            for i in range(2):
                DMA("pool", Wo[i], wview(d_wout, i * 512, (i + 1) * 512))
            x1 = [sub(x1_all, lambda a, n=n: a[:, n], "x1_%d" % n) for n in range(NT)]
            h2T = [sub(h2T_all, lambda a, n=n: a[:, :, n * 128:(n + 1) * 128], "h2T%d" % n) for n in range(NT)]
            junk = AR.tile("junk", BF16, [128, D])
            tmpn = AR.tile("tmpn", F32, [128, D])
            rstd2 = AR.tile("rstd2", F32, [128, 2])
            tmpx = [AR.tile("tmpx%d" % i, F32, [128, D]) for i in range(2)]

            def stageX(n):
                xt = xts[n % 2]
                DMA("sp", xt, d_x[n * 128:(n + 1) * 128, :])
                for cb in range(2):
                    bk = banks[(2 * n + cb) % 4]
                    for dc in range(8):
                        MM(bk, Reg(mT_all.ap[:, dc, n * 128:(n + 1) * 128], mT[(dc, n // 4)].bufs), Wo[cb][:, dc, :], dc == 0, dc == 7)
                    TT("dve", tmpx[n % 2][:, cb * 512:(cb + 1) * 512], bk, mx2[:, cb * 512:(cb + 1) * 512], ALU.mult)
                TT("dve", x1[n], tmpx[n % 2], xt, ALU.add)

            hxd = [hx, AR.tile("hxb", BF16, [128, D])]

            def stageY1(n):
                ACT(junk, x1[n], AF.Square, accum_out=ss[:, 1:2])
                ACT(lnv[:, 1:2], ss[:, 1:2], AF.Ln, scale=1.0 / D, bias=EPS)
                ACT(rstd2[:, 0:1], lnv[:, 1:2], AF.Exp, scale=-0.5)
                STT(tmpn, x1[n], rstd2[:, 0:1], A2, ALU.mult, ALU.mult)
                TT("dve", hxd[n % 2], tmpn, B2, ALU.add)

            def stageY2(n):
                bb = bankbf[4 + n % 2]
                for kc in range(8):
                    TR(bb[:, kc * 128:(kc + 1) * 128], hxd[n % 2][:, kc * 128:(kc + 1) * 128], ident)
                CP("act", h2T[n], bb.v(lambda a: a.rearrange("p (a b) -> p a b", a=8)))

            stageX(0)
            for n in range(NT):
                if n + 1 < NT:
                    stageX(n + 1)
                stageY1(n)
                if n >= 1:
                    stageY2(n - 1)
            stageY2(NT - 1)
            if debug == "p3b":
                P.barrier()
                dump("x1", x1_all, [128, NT, D])
                dump("h2T", h2T_all, [128, 8, SEQ], BF16)
                raise _Stop()
            Wu = [[None, None], [None, None]]
            Dg = [[None, None], [None, None]]
            for hf in range(2):
                Wu[0][hf] = AR.tile("Wu0%d" % hf, BF16, [128, 8, 128])
                Dg[0][hf] = AR.tile("Dg0%d" % hf, BF16, [128, 9, 128])
                DMA("pool", Wu[0][hf], wview(d_wup, hf * DFF, hf * DFF + 128))
                for k in range(9):
                    TS("pool", Dg[0][hf][:, k, :], ident, convw[:, hf * NFF, k:k + 1], 1.0, ALU.mult, ALU.mult)
            P.barrier()
            AR.release("Wo0", "Wo1", "mT_all", "junk", "tmpn", "tmpx0", "tmpx1", "hx", "hxb", "mx2", "A2", "B2", "xt0", "xt1")

            fg = AR.tile("fg", F32, [128, D])
            DMA("sp", fg, bcast_row(d_fg))
            GROUPS = [(0, 6), (6, 12), (12, 17), (17, 22)]
            GMAX = 6
            actT_all = AR.tile("actT", BF16, [128, GMAX, SEQ])
            Wd = AR.tile("Wd", BF16, [128, GMAX, D])
            zp = [[AR.tile("zp%d%d" % (i, hf), BF16, [128, 34, 66]) for hf in range(2)] for i in range(2)]
            zpb = [[[Buf("zpb") for tb in range(4)] for hf in range(2)] for i in range(2)]
            for hf in range(2):
                Wu[1][hf] = AR.tile("Wu1%d" % hf, BF16, [128, 8, 128])
                Dg[1][hf] = AR.tile("Dg1%d" % hf, BF16, [128, 9, 128])
            s1 = [AR.tile("s1_%d" % i, BF16, [128, 512]) for i in range(2)]
            tmpd = AR.tile("tmpd", F32, [128, 512])
            yo = [AR.tile("yo%d" % i, F32, [128, D]) for i in range(2)]
            junk = AR.tile("junk", BF16, [128, D])
            for i in range(2):
                for hf in range(2):
                    MEMSET("pool", Reg(zp[i][hf].ap, zp[i][hf].bufs + zpb[i][hf]), 0.0)

            def up(j):
                i = j % 2
                for hf in range(2):
                    if j == 0:
                        continue
                    col0 = hf * DFF + j * 128
                    DMA("pool", Wu[i][hf], wview(d_wup, col0, col0 + 128))
                    for k in range(9):
                        TS("pool", Dg[i][hf][:, k, :], ident, convw[:, hf * NFF + j, k:k + 1], 1.0, ALU.mult, ALU.mult)
                for hf in range(2):
                    for tb in range(4):
                        bk = banks[(hf * 4 + tb) % 2]
                        for kc in range(8):
                            MM(bk, Wu[i][hf][:, kc, :], h2T_all[:, kc, tb * 512:(tb + 1) * 512], kc == 0, kc == 7)
                        dst = Reg(zp[i][hf].ap[:, 1 + tb * 8: 9 + tb * 8, 1:65], [zpb[i][hf][tb]])
                        CP("act" if tb % 2 == 0 else "dve", dst, bk.v(lambda a: a.rearrange("p (r w) -> p r w", r=8)))

            def conv(j, jl):
                i = j % 2
                for tb in range(4):
                    nb = [t for t in (tb - 1, tb, tb + 1) if 0 <= t < 4]
                    kc1 = banks[2 + tb % 2]
                    kc2 = banks[4 + tb % 2]
                    for hf, bk in ((0, kc1), (1, kc2)):
                        src_bufs = zp[i][hf].bufs + [zpb[i][hf][t] for t in nb]
                        for k in range(9):
                            dr, dw = k // 3, k % 3
                            rhs = Reg(zp[i][hf].ap[:, tb * 8 + dr: tb * 8 + dr + 8, dw: dw + 64], src_bufs)
                            MM(bk, Dg[i][hf][:, k, :], rhs, k == 0, k == 8)
                    ACT(s1[tb % 2], kc1, AF.Silu, bias=convb[:, j:j + 1])
                    STT(Reg(actT_all.ap[:, jl, tb * 512:(tb + 1) * 512], actT_bufs[jl]), kc2, convb[:, NFF + j:NFF + j + 1], s1[tb % 2], ALU.add, ALU.mult)

            actT_bufs = [[Buf("actT%d" % g)] for g in range(GMAX)]
            up(0)
            for (g0, g1) in GROUPS:
                for j in range(g0, g1):
                    if j + 1 < NFF:
                        up(j + 1)
                    conv(j, j - g0)
                    if j == g0 + 1:
                        DMA("pool", Wd[:, 0:g1 - g0, :], d_wdown[g0 * 128:g1 * 128, :].v(lambda a: a.rearrange("(j p) n -> p j n", p=128)))
                        TT("pool", Wd[:, 0:g1 - g0, :], Wd[:, 0:g1 - g0, :], mx5.v(lambda a, g=g1 - g0: a.unsqueeze(1).to_broadcast([128, g, D])), ALU.mult)
                for n in range(NT):
                    for cb in range(2):
                        bk = banks[(6, 7, 2, 3, 4, 5)[(2 * n + cb) % 6]]
                        for jl in range(g1 - g0):
                            MM(bk, Reg(actT_all.ap[:, jl, n * 128:(n + 1) * 128], actT_bufs[jl]), Wd[:, jl, cb * 512:(cb + 1) * 512], jl == 0, jl == g1 - g0 - 1)
                        TT("dve", x1[n][:, cb * 512:(cb + 1) * 512], bk, x1[n][:, cb * 512:(cb + 1) * 512], ALU.add)
                    if g1 == NFF:
                        ACT(junk, x1[n], AF.Square, accum_out=ss[:, 0:1])
                        ACT(lnv[:, 0:1], ss[:, 0:1], AF.Ln, scale=1.0 / D, bias=EPS)
                        ACT(rstd2[:, 1:2], lnv[:, 0:1], AF.Exp, scale=-0.5)
                        STT(yo[n % 2], x1[n], rstd2[:, 1:2], fg, ALU.mult, ALU.mult)
                        DMA("sp", d_out[n * 128:(n + 1) * 128, :], yo[n % 2])


        try:
            body()
        except _Stop:
            pass
        P = hold['P']
        AR = hold['AR']
        P.emit()
        build.stats = dict(P.stats, arena_peak=AR.peak)
    return nc, dbg_out


def _consts():
    bf = ml_dtypes.bfloat16
    s = np.arange(128)[:, None]
    t = np.arange(128)[None, :]
    c = {}
    c["ident"] = np.eye(128, dtype=np.float32).astype(bf)
    c["trimf"] = ((s <= t).astype(np.float32) - (s <= 63).astype(np.float32))
    c["trimb"] = ((s >= t).astype(np.float32) - (s >= 64).astype(np.float32))
    c["maskf"] = np.tile((s <= t).astype(np.float32), (1, 4)).astype(bf)
    c["maskb"] = np.tile((s >= t).astype(np.float32), (1, 4)).astype(bf)
    sv = np.arange(128)
    c["self"] = np.stack([np.ones(128), sv <= 63, sv > 63, np.zeros(128)], axis=1).astype(np.float32)
    c["selb"] = np.stack([np.ones(128), sv >= 64, sv < 64, np.zeros(128)], axis=1).astype(np.float32)
    c["ones_bf"] = np.ones((128, 128), np.float32).astype(bf)
    ang = 2.0 * np.pi * ((s * t) % 128) / 128.0
    c["cc"] = (np.cos(ang) / 512.0).astype(np.float32).astype(bf)
    c["scn"] = (-np.sin(ang) / 512.0).astype(np.float32).astype(bf)
    c["scp"] = (np.sin(ang) / 512.0).astype(np.float32).astype(bf)
    l = np.arange(SEQ, dtype=np.int64)
    kk = np.arange(1, SEQ // 2 + 1, dtype=np.int64)
    angL = 2.0 * np.pi * ((l[:, None] * kk[None, :]) % SEQ) / SEQ
    c["dftc"] = np.cos(angL).astype(np.float32).astype(bf)
    c["dfts"] = np.sin(angL).astype(np.float32).astype(bf)
    return c


_CACHE = {}


def kernel(x, c, ctx, c_ctx, ada_w, ada_b, norm1_g, w_in, hg_lb, hg_onorm_g, w_a, w_b, w_out,
           norm2_g, ffn_up, ffn_conv_w, ffn_conv_b, ffn_down, final_g, _debug=None):
    f = lambda a: np.ascontiguousarray(np.asarray(a, dtype=np.float32))
    x = f(x); c = f(c); ctx = f(ctx); c_ctx = f(c_ctx)
    key = _debug
    if key not in _CACHE:
        _CACHE[key] = (build(_debug), _consts())
    (nc, dbg_out), consts = _CACHE[key]
    shared = dict(
        ada_w=f(ada_w)[0], ada_b=f(ada_b)[0], norm1_g=f(norm1_g)[0], w_in=f(w_in)[0], hg_lb=f(hg_lb),
        og=np.ascontiguousarray(f(hg_onorm_g)[0].reshape(4, 128).T),
        og_row=f(hg_onorm_g)[0],
        w_a=f(w_a)[0], w_b=f(w_b)[0], w_out=f(w_out)[0], norm2_g=f(norm2_g)[0], ffn_up=f(ffn_up)[0],
        convw=np.ascontiguousarray(f(ffn_conv_w)[0].reshape(9, 2 * NFF, 128).transpose(2, 1, 0)),
        convb=np.ascontiguousarray(f(ffn_conv_b)[0].reshape(2 * NFF, 128).T),
        ffn_down=f(ffn_down)[0], final_g=f(final_g),
    )
    shared.update(consts)
    ccx = c_ctx.reshape(8, 128).T
    in_maps = []
    for b in range(8):
        m = dict(shared)
        m["x"] = x[b]
        m["ctx"] = ctx[b]
        m["ccol"] = np.ascontiguousarray(np.concatenate([c[b].reshape(8, 128).T, ccx], axis=1))
        in_maps.append(m)
    res = run_bass_kernel_spmd(nc, in_maps, core_ids=list(range(8)))
    out = np.stack([np.asarray(res.results[b]["out"], dtype=np.float32) for b in range(8)], axis=0)
    if _debug is not None:
        return out, res.results
    return out
```

```python
import contextlib
import numpy as np
import ml_dtypes
import concourse.bass as bass
import concourse.mybir as mybir
from concourse.bass_utils import run_bass_kernel_spmd

F32 = mybir.dt.float32
BF16 = mybir.dt.bfloat16
AF = mybir.ActivationFunctionType
ALU = mybir.AluOpType

D = 1024
SEQ = 2048
CTX = 256
NT = SEQ // 128
NCH = NT + 2
DFF = 2816
NFF = DFF // 128
EPS = 1e-6

DEBUG = None


class Buf:
    __slots__ = ("name", "last_write", "readers", "excl")

    def __init__(self, name="", excl=False):
        self.name = name
        self.last_write = None
        self.readers = []
        self.excl = excl


class Reg:
    __slots__ = ("ap", "bufs")

    def __init__(self, ap, bufs):
        self.ap = ap
        self.bufs = list(bufs)

    def __getitem__(self, key):
        return Reg(self.ap[key], self.bufs)

    def v(self, fn):
        return Reg(fn(self.ap), self.bufs)


class _Op:
    __slots__ = ("eng", "fn", "deps", "seq", "signal", "is_dma", "dma_sem", "dma_val", "waits", "idx", "attach_ok", "rdeps", "pe_attach")


class Prog:
    ENGS = ("pe", "act", "dve", "pool", "sp")

    def __init__(self, nc, n_dma_sems=48):
        self.nc = nc
        self.ops = []
        self.n_dma_sems = n_dma_sems
        self.eng_obj = {"pe": nc.tensor, "act": nc.scalar, "dve": nc.vector, "pool": nc.gpsimd, "sp": nc.sync}
        self.last_on = {e: None for e in self.ENGS}
        self.dma_since_barrier = []

    def op(self, eng, fn, reads=(), writes=(), dma=False, extra_deps=(), attach_ok=None):
        o = _Op()
        o.eng = eng
        o.attach_ok = (eng in ("dve", "pool")) if attach_ok is None else attach_ok
        o.fn = fn
        o.is_dma = dma
        o.idx = len(self.ops)
        deps = set()
        for b in writes:
            if b.last_write is not None:
                deps.add(b.last_write)
            for r in b.readers:
                deps.add(r)
        if not dma and eng == "pe":
            deps = {d for d in deps if self.ops[d].is_dma or self.ops[d].eng != eng}
        rdeps = set()
        for b in reads:
            if b.last_write is not None:
                deps.add(b.last_write)
                rdeps.add(b.last_write)
            if b.excl:
                for r in b.readers:
                    if self.ops[r].eng != eng:
                        deps.add(r)
        deps.update(extra_deps)
        o.deps = deps
        o.rdeps = rdeps
        o.pe_attach = False
        o.signal = False
        self.ops.append(o)
        for b in reads:
            b.readers.append(o.idx)
        for b in writes:
            b.last_write = o.idx
            b.readers = []
        if dma:
            self.dma_since_barrier.append(o.idx)
        elif fn is not None:
            self.last_on[eng] = o.idx
        return o.idx

    def barrier(self):
        deps = [v for v in self.last_on.values() if v is not None] + list(self.dma_since_barrier)
        self.dma_since_barrier = []
        for e in self.ENGS:
            self.op(e, None, extra_deps=deps)

    def emit(self, final_wait_eng="sp"):
        nc = self.nc
        ops = self.ops
        seqc = {e: 0 for e in self.ENGS}
        for o in ops:
            if not o.is_dma:
                seqc[o.eng] += 1
                o.seq = seqc[o.eng]
        dma_prev = [None] * self.n_dma_sems
        dma_cnt = [0] * self.n_dma_sems
        half = self.n_dma_sems // 2
        kq = {"sp": 0, "pool": 0}
        for o in ops:
            if o.is_dma:
                if o.eng == "sp":
                    j = kq["sp"] % half
                    kq["sp"] += 1
                else:
                    j = half + kq["pool"] % half
                    kq["pool"] += 1
                o.dma_sem = j
                dma_cnt[j] += 16
                o.dma_val = dma_cnt[j]
                if dma_prev[j] is not None:
                    o.deps.add(dma_prev[j])
                dma_prev[j] = o.idx
        know = {e: {x: 0 for x in self.ENGS} for e in self.ENGS}
        know_dma = {e: {} for e in self.ENGS}
        snap = {}
        for o in ops:
            E = o.eng
            waits = []
            comp = {}
            if E == "pe" and o.fn is not None and not o.is_dma:
                raw_pending = False
                for d in o.rdeps:
                    od = ops[d]
                    if od.is_dma:
                        if know_dma[E].get(od.dma_sem, 0) < od.dma_val:
                            raw_pending = True
                    elif know[E][od.eng] < od.seq:
                        raw_pending = True
                o.pe_attach = not raw_pending
            for d in o.deps:
                od = ops[d]
                if od.is_dma:
                    if know_dma[E].get(od.dma_sem, 0) < od.dma_val:
                        waits.append(("dma", od.dma_sem, od.dma_val))
                        know_dma[E][od.dma_sem] = od.dma_val
                else:
                    cur = comp.get(od.eng)
                    if cur is None or ops[cur].seq < od.seq:
                        comp[od.eng] = d
            for e2, d in sorted(comp.items(), key=lambda kv: -ops[kv[1]].seq):
                od = ops[d]
                if know[E][e2] >= od.seq:
                    continue
                od.signal = True
                waits.append(("eng", d))
                know[E][e2] = od.seq
                sn = snap[d]
                for x in self.ENGS:
                    if sn[x] > know[E][x]:
                        know[E][x] = sn[x]
            o.waits = waits
            if not o.is_dma:
                snap[o.idx] = dict(know[E])
        sigc = {e: 0 for e in self.ENGS}
        semval = {}
        for o in ops:
            if not o.is_dma and o.signal:
                sigc[o.eng] += 1
                semval[o.idx] = sigc[o.eng]
        self.stats = dict(n_ops=len(ops), n_sig=dict(sigc), n_waits=sum(len(o.waits) for o in ops))
        with contextlib.ExitStack() as st:
            esem = {e: st.enter_context(nc.semaphore("s_" + e)) for e in self.ENGS}
            dsem = [st.enter_context(nc.semaphore("d%d" % j)) for j in range(self.n_dma_sems)]
            for o in ops:
                eng = self.eng_obj[o.eng]
                waits = list(o.waits)
                attach = waits.pop() if (waits and o.fn is not None and (o.is_dma or o.attach_ok or o.pe_attach)) else None
                for w in waits:
                    if w[0] == "dma":
                        eng.wait_ge(dsem[w[1]], w[2])
                    else:
                        od = ops[w[1]]
                        eng.wait_ge(esem[od.eng], semval[od.idx])
                if o.fn is None:
                    assert not o.signal
                    continue
                inst = o.fn()
                if attach is not None:
                    if attach[0] == "dma":
                        inst._wait_ge(dsem[attach[1]], attach[2])
                    else:
                        od = ops[attach[1]]
                        inst._wait_ge(esem[od.eng], semval[od.idx])
                if o.is_dma:
                    inst.then_inc(dsem[o.dma_sem], 16)
                elif o.signal:
                    inst.then_inc(esem[o.eng], 1)
            eng = self.eng_obj[final_wait_eng]
            for j in range(self.n_dma_sems):
                if dma_cnt[j] > 0:
                    eng.wait_ge(dsem[j], dma_cnt[j])


class Arena:
    def __init__(self, nc, st, nbytes):
        self.nbytes = nbytes
        self.t = st.enter_context(nc.sbuf_tensor("arena", [128, nbytes // 4], F32))
        self.free = [(0, nbytes)]
        self.regs = {}
        self.peak = 0

    def alloc(self, name, nbytes, hi=False):
        nbytes = (nbytes + 63) // 64 * 64
        assert name not in self.regs, name
        if hi:
            for i in range(len(self.free) - 1, -1, -1):
                o, s = self.free[i]
                if s >= nbytes:
                    if s == nbytes:
                        self.free.pop(i)
                    else:
                        self.free[i] = (o, s - nbytes)
                    self.regs[name] = (o + s - nbytes, nbytes)
                    return o + s - nbytes
            raise MemoryError("arena full (hi) allocating %s (%d B); regs=%s" % (name, nbytes, sorted(self.regs.items(), key=lambda kv: kv[1])))
        for i, (o, s) in enumerate(self.free):
            if s >= nbytes:
                if s == nbytes:
                    self.free.pop(i)
                else:
                    self.free[i] = (o + nbytes, s - nbytes)
                self.regs[name] = (o, nbytes)
                self.peak = max(self.peak, o + nbytes)
                return o
        raise MemoryError("arena full allocating %s (%d B); regs=%s" % (name, nbytes, sorted(self.regs.items(), key=lambda kv: kv[1])))

    def release(self, *names):
        for name in names:
            o, s = self.regs.pop(name)
            self.free.append((o, s))
        self.free.sort()
        merged = []
        for o, s in self.free:
            if merged and merged[-1][0] + merged[-1][1] == o:
                merged[-1] = (merged[-1][0], merged[-1][1] + s)
            else:
                merged.append((o, s))
        self.free = merged

    def tile(self, name, dtype, shape, hi=False):
        esz = 4 if dtype == F32 else 2
        n = int(np.prod(shape[1:]))
        o = self.alloc(name, n * esz, hi)
        ap = self.t[:, o // 4: o // 4 + (n * esz + 3) // 4]
        if dtype != F32:
            ap = ap.bitcast(dtype)[:, 0:n]
        if len(shape) == 3:
            ap = ap.rearrange("p (a b) -> p a b", a=shape[1])
        elif len(shape) == 4:
            ap = ap.rearrange("p (a b c) -> p a b c", a=shape[1], b=shape[2])
        return Reg(ap, [Buf(name)])


def _bufs(*regs):
    out = []
    for r in regs:
        if isinstance(r, Reg):
            out.extend(r.bufs)
    return out


def sub(reg, ap_fn, name=None):
    return Reg(ap_fn(reg.ap), [Buf(name or "sub")])


def build(debug=None):
    nc = bass.Bass("TRN2", target_bir_lowering=False)

    def din(name, shape, dt=F32):
        return Reg(nc.dram_tensor(name, list(shape), dt, kind="ExternalInput").ap(), [])

    d_x = din("x", [SEQ, D])
    d_ctx = din("ctx", [CTX, D])
    d_ccol = din("ccol", [128, 16])
    d_adaw = din("ada_w", [D, 6 * D])
    d_adab = din("ada_b", [6 * D])
    d_n1g = din("norm1_g", [D])
    d_win = din("w_in", [D, 5120])
    d_hglb = din("hg_lb", [2, 512])
    d_og = din("og", [128, 4])
    d_ogrow = din("og_row", [512])
    d_wa = din("w_a", [512, D])
    d_wb = din("w_b", [512, D])
    d_wout = din("w_out", [D, D])
    d_n2g = din("norm2_g", [D])
    d_wup = din("ffn_up", [D, 2 * DFF])
    d_convw = din("convw", [128, 2 * NFF, 9])
    d_convb = din("convb", [128, 2 * NFF])
    d_wdown = din("ffn_down", [DFF, D])
    d_fg = din("final_g", [D])
    d_ident = din("ident", [128, 128], BF16)
    d_trimf = din("trimf", [128, 128])
    d_trimb = din("trimb", [128, 128])
    d_maskf = din("maskf", [128, 512], BF16)
    d_maskb = din("maskb", [128, 512], BF16)
    d_self = din("self", [128, 4])
    d_selb = din("selb", [128, 4])
    d_ones = din("ones_bf", [128, 128], BF16)
    d_cc = din("cc", [128, 128], BF16)
    d_scn = din("scn", [128, 128], BF16)
    d_dftc = din("dftc", [SEQ, SEQ // 2], BF16)
    d_dfts = din("dfts", [SEQ, SEQ // 2], BF16)
    d_scp = din("scp", [128, 128], BF16)
    d_out = Reg(nc.dram_tensor("out", [SEQ, D], F32, kind="ExternalOutput").ap(), [Buf("out")])
    dbg_out = {}

    st = contextlib.ExitStack()
    with st:
        class _Stop(Exception):
            pass

        hold = {}

        def body():
            AR = Arena(nc, st, 212000 // 64 * 64)
            banks = []
            for i in range(8):
                t = st.enter_context(nc.psum_tensor("bank%d" % i, [128, 512], F32))
                banks.append(Reg(t[:], [Buf("bank%d" % i, excl=True)]))
            bankbf = [Reg(b.ap.bitcast(BF16), b.bufs) for b in banks]
            P = Prog(nc)
            hold['P'] = P
            hold['AR'] = AR

            def A(x):
                return x.ap if isinstance(x, Reg) else x

            def ACT(out, in_, func, accum_out=None, **kw):
                kwap = {k: A(v) for k, v in kw.items()}
                if accum_out is not None:
                    kwap["accum_out"] = accum_out.ap
                P.op("act", lambda: nc.scalar.activation(out=out.ap, in_=in_.ap, func=func, **kwap),
                     reads=_bufs(in_, *kw.values()), writes=_bufs(out, accum_out), attach_ok=(accum_out is None))

            def TT(eng, out, in0, in1, op):
                e = nc.vector if eng == "dve" else nc.gpsimd
                P.op(eng, lambda: e.tensor_tensor(out=out.ap, in0=in0.ap, in1=in1.ap, op=op),
                     reads=_bufs(in0, in1), writes=_bufs(out))

            def STT(out, in0, scalar, in1, op0, op1):
                P.op("dve", lambda: nc.vector.scalar_tensor_tensor(out=out.ap, in0=in0.ap, scalar=A(scalar), in1=in1.ap, op0=op0, op1=op1),
                     reads=_bufs(in0, scalar, in1), writes=_bufs(out))

            def TS(eng, out, in0, s1, s2, op0, op1):
                e = nc.vector if eng == "dve" else nc.gpsimd
                P.op(eng, lambda: e.tensor_scalar(out=out.ap, in0=in0.ap, scalar1=A(s1), scalar2=A(s2), op0=op0, op1=op1),
                     reads=_bufs(in0, s1, s2), writes=_bufs(out))

            def CP(eng, out, in_):
                if eng == "act":
                    ACT(out, in_, AF.Copy)
                else:
                    e = nc.vector if eng == "dve" else nc.gpsimd
                    P.op(eng, lambda: e.tensor_copy(out=out.ap, in_=in_.ap), reads=_bufs(in_), writes=_bufs(out))

            def RECIP(out, in_):
                P.op("dve", lambda: nc.vector.reciprocal(out=out.ap, in_=in_.ap), reads=_bufs(in_), writes=_bufs(out))

            def MEMSET(eng, out, val):
                e = nc.vector if eng == "dve" else nc.gpsimd
                P.op(eng, lambda: e.memset(out.ap, val), writes=_bufs(out))

            def MM(out, lhsT, rhs, start, stop):
                P.op("pe", lambda: nc.tensor.matmul(out.ap, lhsT=lhsT.ap, rhs=rhs.ap, start=start, stop=stop),
                     reads=_bufs(lhsT, rhs), writes=_bufs(out))

            def TR(out, in_, ident):
                P.op("pe", lambda: nc.tensor.transpose(out=out.ap, in_=in_.ap, identity=ident.ap),
                     reads=_bufs(in_, ident), writes=_bufs(out))

            def DMA(q, out, in_):
                e = P.eng_obj[q]
                P.op(q, lambda: e.dma_start(out=out.ap, in_=in_.ap), reads=_bufs(in_), writes=_bufs(out), dma=True)

            def bcast_row(dreg):
                return dreg.v(lambda a: a.partition_broadcast(128))

            def wview(dreg, c0, c1):
                return dreg.v(lambda a: a[:, c0:c1].rearrange("(kc p) n -> p kc n", p=128))

            def dump(name, reg, shape, dt=F32):
                P.barrier()
                t = Reg(nc.dram_tensor("dbg_" + name, list(shape), dt, kind="ExternalOutput").ap(), [Buf("dbg")])
                DMA("sp", t, reg)
                dbg_out[name] = t

            ident = AR.tile("ident", BF16, [128, 128])
            trimf = AR.tile("trimf", F32, [128, 128])
            trimb = AR.tile("trimb", F32, [128, 128])
            maskf = AR.tile("maskf", BF16, [128, 512])
            maskb = AR.tile("maskb", BF16, [128, 512])
            self_ = AR.tile("self", F32, [128, 4])
            selb = AR.tile("selb", F32, [128, 4])
            ones_bf = AR.tile("ones_bf", BF16, [128, 128])
            cc = AR.tile("cc", BF16, [128, 128])
            scn = AR.tile("scn", BF16, [128, 128])
            scp = AR.tile("scp", BF16, [128, 128])
            og = AR.tile("og", F32, [128, 4])
            lb_bc = AR.tile("lb_bc", F32, [128, 512])
            lnoml_bc = AR.tile("lnoml_bc", F32, [128, 512])
            convw = AR.tile("convw", F32, [128, 2 * NFF, 9])
            convb = AR.tile("convb", F32, [128, 2 * NFF])
            rstd_all = AR.tile("rstd_all", F32, [128, NCH])
            ss = AR.tile("ss", F32, [128, 2])
            lnv = AR.tile("lnv", F32, [128, 2])
            for t, dsrc in ((ident, d_ident), (trimf, d_trimf), (trimb, d_trimb), (maskf, d_maskf), (maskb, d_maskb),
                            (self_, d_self), (selb, d_selb), (ones_bf, d_ones), (cc, d_cc), (scn, d_scn), (scp, d_scp), (og, d_og),
                            (convw, d_convw), (convb, d_convb)):
                DMA("sp", t, dsrc)

            A1 = AR.tile("A1", F32, [128, D], hi=True)
            B1 = AR.tile("B1", F32, [128, D], hi=True)
            A1c = AR.tile("A1c", F32, [128, D], hi=True)
            B1c = AR.tile("B1c", F32, [128, D], hi=True)

            t0 = AR.tile("p0_t0", F32, [128, 512])
            t1 = AR.tile("p0_t1", F32, [128, 512])
            DMA("sp", t0, bcast_row(d_hglb[0]))
            DMA("sp", t1, bcast_row(d_hglb[1]))
            TT("dve", t0, t0, t1, ALU.subtract)
            ACT(t1, t0, AF.Exp, scale=-1.0)
            TS("dve", t1, t1, 1.0, None, ALU.add, ALU.bypass)
            RECIP(lb_bc, t1)
            ACT(lnoml_bc, lb_bc, AF.Ln, scale=-1.0, bias=1.0)

            ccol = AR.tile("ccol", F32, [128, 16], hi=True)
            csil = AR.tile("csil", F32, [128, 16], hi=True)
            cb_bf = AR.tile("cb_bf", BF16, [128, 16, 128], hi=True)
            DMA("sp", ccol, d_ccol)
            ACT(csil, ccol, AF.Exp, scale=-1.0)
            TS("dve", csil, csil, 1.0, None, ALU.add, ALU.bypass)
            RECIP(csil, csil)
            TT("dve", csil, csil, ccol, ALU.mult)
            CP("dve", cb_bf, csil.v(lambda a: a.unsqueeze(2).to_broadcast([128, 16, 128])))

            wslots = [AR.tile("wslot%d" % i, BF16, [128, 8, 512]) for i in range(2)]
            adab = AR.tile("adab", F32, [128, D])
            gbc = AR.tile("gbc", F32, [128, D])
            mtmp = AR.tile("mtmp", F32, [128, 512])
            wk = [0]

            def mods(j, handler, fixed_bank=None):
                DMA("sp", adab, bcast_row(d_adab[j * D:(j + 1) * D]))
                for hh in range(2):
                    slot = wslots[wk[0] % 2]
                    wk[0] += 1
                    c0 = j * D + hh * 512
                    DMA("pool", slot, wview(d_adaw, c0, c0 + 512))
                    for which in range(2):
                        if not handler(which, None, None, None):
                            continue
                        bk = banks[(wk[0] * 2 + which) % 8] if fixed_bank is None else banks[fixed_bank]
                        for kc in range(8):
                            MM(bk, cb_bf[:, which * 8 + kc, :], slot[:, kc, :], kc == 0, kc == 7)
                        handler(which, hh, bk, adab[:, hh * 512:(hh + 1) * 512])

            DMA("sp", gbc, bcast_row(d_n1g))

            def h_shift(dst_lat, dst_ctx):
                def h(which, hh, bk, ab):
                    dst = dst_lat if which == 0 else dst_ctx
                    if dst is None:
                        return False
                    if hh is None:
                        return True
                    TT("dve", dst[:, hh * 512:(hh + 1) * 512], bk, ab, ALU.add)
                    return True
                return h

            def h_scale(dst_lat, dst_ctx):
                def h(which, hh, bk, ab):
                    dst = dst_lat if which == 0 else dst_ctx
                    if dst is None:
                        return False
                    if hh is None:
                        return True
                    TT("dve", mtmp, bk, ab, ALU.add)
                    STT(dst[:, hh * 512:(hh + 1) * 512], mtmp, 1.0, gbc[:, hh * 512:(hh + 1) * 512], ALU.add, ALU.mult)
                    return True
                return h

            mods(0, h_shift(B1, B1c))
            mods(1, h_scale(A1, A1c))

            if debug == "mods":
                dump("A1", A1, [128, D]); dump("B1", B1, [128, D]); dump("A1c", A1c, [128, D]); dump("B1c", B1c, [128, D])
                dump("lb", lb_bc, [128, 512])
                raise _Stop()

            Wh = [AR.tile("Wh%d" % i, BF16, [128, 8, 512]) for i in range(5)]
            for i in (1, 2, 3):
                DMA("pool", Wh[i], wview(d_win, 512 + i * 512, 1024 + i * 512))
            late = P.dma_since_barrier[-2:]
            P.dma_since_barrier = P.dma_since_barrier[:-2]
            P.barrier()
            P.dma_since_barrier.extend(late)
            for i in (0, 4):
                DMA("pool", Wh[i], wview(d_win, 512 + i * 512, 1024 + i * 512))
            AR.release("wslot0", "wslot1", "adab", "gbc", "mtmp", "p0_t0", "p0_t1")
            sggT = [None] * NT
            QbT = [None] * NT
            opart = [None] * NT
            sgg_all = AR.tile("sggT_all", BF16, [128, NT, 4, 128], hi=True)
            QbT_all = AR.tile("QbT_all", BF16, [128, NT, 4, 128], hi=True)
            opart_all = AR.tile("opart_all", BF16, [128, NT, 4, 128], hi=True)
            bPb_all = AR.tile("bPb_all", BF16, [128, NCH, 4, 128], hi=True)
            abcb_all = AR.tile("abcb_all", F32, [128, NCH, 4, 4], hi=True)
            for n in range(NT):
                sggT[n] = sub(sgg_all, lambda a, n=n: a[:, n], "sgg%d" % n)
                QbT[n] = sub(QbT_all, lambda a, n=n: a[:, n], "QbT%d" % n)
                opart[n] = sub(opart_all, lambda a, n=n: a[:, n], "opart%d" % n)
            bPb = [sub(bPb_all, lambda a, c=c: a[:, c], "bPb%d" % c) for c in range(NCH)]
            abcb = [sub(abcb_all, lambda a, c=c: a[:, c], "abcb%d" % c) for c in range(NCH)]
            rstd_c = [sub(rstd_all, lambda a, c=c: a[:, c:c + 1], "rstd%d" % c) for c in range(NCH)]

            xts = [AR.tile("xt%d" % i, F32, [128, D]) for i in range(2)]
            junk = AR.tile("junk", BF16, [128, D])
            hx = AR.tile("hx", BF16, [128, D])
            hxT = [AR.tile("hxT%d" % i, BF16, [128, 8, 128]) for i in range(2)]
            Ta = AR.tile("Ta", F32, [128, 512])
            q32 = AR.tile("q32", F32, [128, 512])
            Tb = AR.tile("Tb", F32, [128, 512])
            Vbf = AR.tile("Vbf", BF16, [128, 512])
            Tz = [[AR.tile("Tz%d%d" % (z, i), F32, [128, 512]) for i in range(3)] for z in range(2)]
            Qz = [AR.tile("Qz%d" % z, BF16, [128, 512]) for z in range(2)]
            Ktz = [AR.tile("Ktz%d" % z, BF16, [128, 512]) for z in range(2)]
            QfT = AR.tile("QfT", BF16, [128, 4, 128])
            KTz = [AR.tile("KTz%d" % z, BF16, [128, 4, 128]) for z in range(2)]
            Amz = [AR.tile("Am%d" % z, BF16, [128, 512]) for z in range(2)]
            Spf = AR.tile("Spf", BF16, [128, 4, 128])
            tmpS = AR.tile("tmpS", F32, [128, 4, 128])
            Sf = AR.tile("Sf", F32, [128, 4, 128])
            abcf = [AR.tile("abcf%d" % i, F32, [128, 4, 4]) for i in range(2)]
            MEMSET("pool", Sf, 0.0)
            og_bc = og.v(lambda a: a.unsqueeze(2).to_broadcast([128, 4, 128]))
            og_row = AR.tile("og_row", F32, [128, 512])
            DMA("sp", og_row, bcast_row(d_ogrow))
            sggtm = AR.tile("sggtm", BF16, [128, 512])
            trim = (trimf, trimb)
            sel = (self_, selb)
            mask = (maskf, maskb)
            K0, K1, K2, K3, K4, K5, K6, K7 = banks

            def norm_hx(ci, xt, Areg, Breg, have_rstd, hxT_out, tb=0):
                if not have_rstd:
                    ACT(junk, xt, AF.Square, accum_out=ss[:, 0:1])
                    ACT(lnv[:, 0:1], ss[:, 0:1], AF.Ln, scale=1.0 / D, bias=EPS)
                    ACT(rstd_c[ci], lnv[:, 0:1], AF.Exp, scale=-0.5)
                STT(xt, xt, rstd_c[ci], Areg, ALU.mult, ALU.mult)
                TT("dve", hx, xt, Breg, ALU.add)
                for kc in range(8):
                    TR(bankbf[tb][:, kc * 128:(kc + 1) * 128], hx[:, kc * 128:(kc + 1) * 128], ident)
                CP("dve", hxT_out, bankbf[tb].v(lambda a: a.rearrange("p (a b) -> p a b", a=8)))

            WZ = (K6, K0)

            def stageA(ci, part=None):
                lat = ci >= 2
                n = ci - 2
                xt = xts[ci % 2]
                hT = hxT[ci % 2]
                if part in (None, 0):
                    src = d_x[n * 128:(n + 1) * 128, :] if lat else d_ctx[ci * 128:(ci + 1) * 128, :]
                    DMA("sp", xt, src)
                    ACT(junk, xt, AF.Square, accum_out=ss[:, 0:1])
                    ACT(lnv[:, 0:1], ss[:, 0:1], AF.Ln, scale=1.0 / D, bias=EPS)
                    ACT(rstd_c[ci], lnv[:, 0:1], AF.Exp, scale=-0.5)
                    STT(xt, xt, rstd_c[ci], A1 if lat else A1c, ALU.mult, ALU.mult)
                    TT("dve", hx, xt, B1 if lat else B1c, ALU.add)
                if part in (None, 1):
                    for kc in range(8):
                        TR(bankbf[7][:, kc * 128:(kc + 1) * 128], hx[:, kc * 128:(kc + 1) * 128], ident)
                    CP("act", hT, bankbf[7].v(lambda a: a.rearrange("p (a b) -> p a b", a=8)))

            def stageA2(ci, part):
                lat = ci >= 2
                hT = hxT[ci % 2]
                if part == 0:
                    blks = [(1, K2), (2, K3)]
                elif part == 1:
                    blks = [(3, K4)] + ([(0, K1)] if lat else [])
                else:
                    blks = []
                for bi, bk in blks:
                    for kc in range(8):
                        MM(bk, hT[:, kc, :], Wh[bi][:, kc, :], kc == 0, kc == 7)
                if lat and part == 2:
                    for kc in range(8):
                        MM(K5, hT[:, kc, :], Wh[4][:, kc, :], kc == 0, kc == 7)

            T1 = [Tz[0][0], Tz[1][0]]
            T2 = [Tz[0][1], Tz[1][1]]
            T3 = [Tz[0][2], Tz[1][2]]
            Tg = AR.tile("Tg", F32, [128, 512])
            Qz2 = [Qz, [AR.tile("Qzb%d" % z, BF16, [128, 512]) for z in range(2)]]
            Ktz2 = [Ktz, [AR.tile("Ktzb%d" % z, BF16, [128, 512]) for z in range(2)]]
            Vbf2 = [Vbf, AR.tile("Vbfb", BF16, [128, 512])]
            Spf2 = [Spf, AR.tile("Spfb", BF16, [128, 4, 128])]

            def stageBfront(ci):
                lat = ci >= 2
                KM = (K2, K3)
                for z in range(2):
                    ACT(T1[z], KM[z], AF.Exp)
                CP("act", Vbf2[ci % 2], K4)
                if lat:
                    CP("dve", q32, K1)
                    CP("act", Tg, K5)

            def stageB1(ci):
                lat = ci >= 2
                n = ci - 2
                for z in range(2):
                    ACT(T2[z], T1[z], AF.Ln, bias=1.0)
                for z in range(2):
                    TT("dve", T1[z], T1[z], lb_bc, ALU.add)
                for z in range(2):
                    ACT(T1[z], T1[z], AF.Ln)
                for z in range(2):
                    TT("dve", T1[z], T1[z], T2[z], ALU.subtract)
                for z in range(2):
                    MM(WZ[z], trim[z], T1[z], True, True)
                for z in range(2):
                    for h in range(4):
                        MM(K7[:, z * 16 + h * 4: z * 16 + h * 4 + 4], T1[z][:, h * 128:(h + 1) * 128], sel[z], True, True)
                for z in range(2):
                    TT("dve", T2[z], lnoml_bc, T2[z], ALU.subtract)
                if lat:
                    ACT(Ta, q32, AF.Exp, scale=-1.0)
                    ACT(Tb, Tg, AF.Exp, scale=-1.0)
                    ACT(Ta, Ta, AF.Ln, bias=1.0)
                    ACT(Tb, Tb, AF.Ln, bias=1.0)
                    ACT(Ta, Ta, AF.Exp, scale=-1.0)
                    ACT(Tb, Tb, AF.Exp, scale=-1.0)
                    TT("dve", q32, q32, Ta, ALU.mult)
                    TT("dve", Tb, Tg, Tb, ALU.mult)
                    TT("dve", sggtm, Tb, og_row, ALU.mult)

            def stageGT(ci):
                n = ci - 2
                if n < 0:
                    return
                for h in range(4):
                    TR(bankbf[5][:, h * 128:(h + 1) * 128], sggtm[:, h * 128:(h + 1) * 128], ident)
                CP("act", sggT[n], bankbf[5][:, 0:512].v(lambda a: a.rearrange("p (h t) -> p h t", h=4)))

            def stageB2(ci):
                lat = ci >= 2
                n = ci - 2
                abc = [abcf[ci % 2], abcb[ci]]
                Qc, Kc, Vc = Qz2[ci % 2], Ktz2[ci % 2], Vbf2[ci % 2]
                for z in range(2):
                    ACT(abc[z].v(lambda a: a.rearrange("p h c -> p (h c)")), K7[:, z * 16: z * 16 + 16], AF.Exp)
                for z in range(2):
                    if lat:
                        ACT(T3[z], WZ[z], AF.Exp)
                    TT("dve", T2[z], T2[z], WZ[z], ALU.subtract)
                for z in range(2):
                    ACT(Kc[z], T2[z], AF.Exp)
                    if lat:
                        TT("dve", Qc[z], q32, T3[z], ALU.mult)
                for z in range(2):
                    for h in range(4):
                        MM(WZ[z][:, h * 128:(h + 1) * 128], Kc[z][:, h * 128:(h + 1) * 128], Vc[:, h * 128:(h + 1) * 128], True, True)
                if lat:
                    TT("dve", Spf2[ci % 2], Sf, abcf[ci % 2].v(lambda a: a[:, :, 1:2].to_broadcast([128, 4, 128])), ALU.mult)
                bf_bc = abcf[ci % 2].v(lambda a: a[:, :, 2:3].to_broadcast([128, 4, 128]))
                bb_bc = abcb[ci].v(lambda a: a[:, :, 2:3].to_broadcast([128, 4, 128]))
                TT("dve", tmpS, K6.v(lambda a: a.rearrange("p (h t) -> p h t", h=4)), bf_bc, ALU.mult)
                for h in range(4):
                    STT(Sf[:, h, :], Sf[:, h, :], abcf[ci % 2][:, h, 0:1], tmpS[:, h, :], ALU.mult, ALU.add)
                TT("dve", bPb[ci], K0.v(lambda a: a.rearrange("p (h t) -> p h t", h=4)), bb_bc, ALU.mult)

            KA = (K4, K1)

            def stageB3(ci, part):
                n = ci - 2
                if n < 0:
                    return
                Qc, Kc, Vc, Sp = Qz2[ci % 2], Ktz2[ci % 2], Vbf2[ci % 2], Spf2[ci % 2]
                tbank = (bankbf[7], bankbf[6])
                QT = (QfT, QbT[n])
                if part == 0:
                    for z in range(2):
                        for h in range(4):
                            TR(tbank[z][:, h * 128:(h + 1) * 128], Qc[z][:, h * 128:(h + 1) * 128], ident)
                        for h in range(4):
                            TR(tbank[z][:, 512 + h * 128: 512 + (h + 1) * 128], Kc[z][:, h * 128:(h + 1) * 128], ident)
                    for z in range(2):
                        CP("dve", QT[z], tbank[z][:, 0:512].v(lambda a: a.rearrange("p (h t) -> p h t", h=4)))
                        CP("act", KTz[z], tbank[z][:, 512:1024].v(lambda a: a.rearrange("p (h t) -> p h t", h=4)))
                elif part == 1:
                    for z in range(2):
                        for h in range(4):
                            MM(KA[z][:, h * 128:(h + 1) * 128], KTz[z][:, h, :], QT[z][:, h, :], True, True)
                    for z in range(2):
                        TT("dve", Amz[z], KA[z], mask[z], ALU.mult)
                else:
                    for h in range(4):
                        o_h = K5[:, h * 128:(h + 1) * 128]
                        MM(o_h, Vc[:, h * 128:(h + 1) * 128], Amz[0][:, h * 128:(h + 1) * 128], True, False)
                        MM(o_h, Vc[:, h * 128:(h + 1) * 128], Amz[1][:, h * 128:(h + 1) * 128], False, False)
                        MM(o_h, Sp[:, h, :], QfT[:, h, :], False, True)
                    CP("act", opart[n], K5.v(lambda a: a.rearrange("p (h t) -> p h t", h=4)))

            stageA(0)
            for part in range(3):
                stageA2(0, part)
            stageA(1)
            for ci in range(NCH):
                nxt = ci + 1 < NCH
                if ci + 2 < NCH:
                    stageA(ci + 2, 0)
                stageBfront(ci)
                stageB3(ci - 1, 0)
                if nxt:
                    stageA2(ci + 1, 0)
                stageB3(ci - 1, 1)
                stageB1(ci)
                stageB3(ci - 1, 2)
                if nxt:
                    stageA2(ci + 1, 1)
                stageB2(ci)
                stageGT(ci)
                if nxt:
                    stageA2(ci + 1, 2)
                if ci + 2 < NCH:
                    stageA(ci + 2, 1)
            for part in range(3):
                stageB3(NCH - 1, part)

            if debug == "p1":
                dump("Sf", Sf, [128, 4, 128])
                dump("opart", opart_all, [128, NT, 4, 128], BF16)
                dump("sgg", sgg_all, [128, NT, 4, 128], BF16)
                dump("abcb", abcb_all, [128, NCH, 4, 4])
                dump("bPb", bPb_all, [128, NCH, 4, 128], BF16)
                dump("rstd", rstd_all, [128, NCH])
                raise _Stop()

            P.barrier()
            AR.release("xt0", "xt1", "junk", "hx", "hxT0", "hxT1", "Ta", "q32", "Tb", "Vbf", "Tz00", "Tz01", "Tz02", "Tz10", "Tz11", "Tz12",
                       "Qz0", "Qz1", "Ktz0", "Ktz1", "QfT", "Tg", "og_row", "sggtm", "Qzb0", "Qzb1", "Ktzb0", "Ktzb1", "Vbfb", "Spfb", "KTz0", "KTz1", "Am0", "Am1", "Spf", "tmpS", "Sf", "abcf0", "abcf1",
                       "Wh0", "Wh1", "Wh2", "Wh3", "Wh4", "A1c", "B1c")

            yhT_all = AR.tile("yhT_all", BF16, [128, 4, SEQ], hi=True)
            xts = [AR.tile("xt%d" % i, F32, [128, D]) for i in range(2)]
            hx = AR.tile("hx", BF16, [128, D])
            mx2 = AR.tile("mx2", F32, [128, D])
            A2 = AR.tile("A2", F32, [128, D])
            B2 = AR.tile("B2", F32, [128, D])
            mx5 = AR.tile("mx5", F32, [128, D])
            yhT = [sub(yhT_all, lambda a, n=n: a[:, :, n * 128:(n + 1) * 128], "yhT%d" % n) for n in range(NT)]
            Sb = AR.tile("Sb", F32, [128, 4, 128])
            Spb_all = AR.tile("Spb_all", BF16, [128, NT, 4, 128])
            Spb = [sub(Spb_all, lambda a, n=n: a[:, n], "Spb%d" % n) for n in range(NT)]
            ot = [AR.tile("ot%d" % i, F32, [128, 512]) for i in range(2)]
            sq = [AR.tile("sq%d" % i, BF16, [128, 512]) for i in range(2)]
            rr = [AR.tile("rr%d" % i, F32, [128, 512]) for i in range(2)]
            Sb_h = [sub(Sb, lambda a, h=h: a[:, h, :], "Sb_h%d" % h) for h in range(4)]
            for h in range(4):
                MEMSET("pool", Sb_h[h], 0.0)
            Wf = AR.tile("Wf", BF16, [128, 8, 512])
            DMA("pool", Wf, wview(d_win, 0, 512))
            U_all = AR.tile("U_all", BF16, [128, NT, 512])
            U = [sub(U_all, lambda a, n=n: a[:, n], "U%d" % n) for n in range(NT)]
            hxT = [AR.tile("hxT%d" % i, BF16, [128, 8, 128], hi=True) for i in range(2)]

            def sweep_tile(n):
                xt = xts[n % 2]
                DMA("sp", xt, d_x[n * 128:(n + 1) * 128, :])
                norm_hx(n + 2, xt, A1, B1, True, hxT[n % 2], tb=4)
                bk = banks[5 + n % 2]
                for kc in range(8):
                    MM(bk, hxT[n % 2][:, kc, :], Wf[:, kc, :], kc == 0, kc == 7)
                CP("act", U[n], bk)

            def sb_update(ci):
                for h in range(4):
                    STT(Sb_h[h], Sb_h[h], abcb[ci][:, h, 0:1], bPb[ci][:, h, :], ALU.mult, ALU.add)

            sb_update(1)
            sb_update(0)
            if debug == "p2":
                dsb = AR.tile("dbg_sb", F32, [128, 4, 128])
                for h in range(4):
                    CP("dve", dsb[:, h, :], Sb_h[h])
                dump("Sb_ctx", dsb, [128, 4, 128])
            ot4 = ot + [AR.tile("ot%d" % i, F32, [128, 512]) for i in (2, 3)]
            hxs = [hx, AR.tile("hxs", BF16, [128, D])]

            def S1(k):
                xt = xts[k % 2]
                DMA("sp", xt, d_x[k * 128:(k + 1) * 128, :])
                STT(xt, xt, rstd_c[k + 2], A1, ALU.mult, ALU.mult)
                TT("dve", hxs[k % 2], xt, B1, ALU.add)

            def S2(k):
                for kc in range(8):
                    TR(bankbf[4][:, kc * 128:(kc + 1) * 128], hxs[k % 2][:, kc * 128:(kc + 1) * 128], ident)
                CP("act", hxT[k % 2], bankbf[4].v(lambda a: a.rearrange("p (a b) -> p a b", a=8)))

            def S3(k):
                bk = banks[5 + k % 2]
                for kc in range(8):
                    MM(bk, hxT[k % 2][:, kc, :], Wf[:, kc, :], kc == 0, kc == 7)
                CP("act", U[k], bk)

            def chain_step(i):
                n = NT - 1 - i
                ci = n + 2
                for h in range(4):
                    TS("dve", Spb[n][:, h, :], Sb_h[h], abcb[ci][:, h, 1:2], None, ALU.mult, ALU.bypass)
                sb_update(ci)

            chain_step(0)
            chain_step(1)

            def R1a(c):
                ko = banks[0 + 2 * (c % 2)]
                for h in range(4):
                    MM(ko[:, h * 128:(h + 1) * 128], Spb[c][:, h, :], QbT[c][:, h, :], True, True)
                TT("dve", ot4[c % 4], ko, opart[c].v(lambda a: a.rearrange("p h t -> p (h t)")), ALU.add)

            def R1b(c):
                km = banks[1 + 2 * (c % 2)]
                ACT(sq[c % 2], ot4[c % 4], AF.Square)
                MM(km, ones_bf, sq[c % 2], True, True)

            def R2a(c):
                km = banks[1 + 2 * (c % 2)]
                ACT(rr[c % 2], km, AF.Ln, scale=1.0 / 128, bias=EPS)
                ACT(rr[c % 2], rr[c % 2], AF.Exp, scale=-0.5)

            def R2b(c):
                TT("dve", ot4[c % 4], ot4[c % 4], rr[c % 2], ALU.mult)
                TT("dve", yhT[c], ot4[c % 4].v(lambda a: a.rearrange("p (h t) -> p h t", h=4)), sggT[c], ALU.mult)

            for i in range(NT + 3):
                cs = [NT - 1 - (i - d) for d in range(4)]
                if i + 2 < NT:
                    chain_step(i + 2)
                if i < NT:
                    S1(i)
                if 0 <= i - 1 < NT:
                    S2(i - 1)
                if 0 <= i - 2 < NT:
                    S3(i - 2)
                if 0 <= cs[0] < NT and i < NT:
                    R1a(cs[0])
                if 0 <= cs[1] < NT and 0 <= i - 1 < NT:
                    R1b(cs[1])
                if 0 <= cs[2] < NT and 0 <= i - 2 < NT:
                    R2a(cs[2])
                if 0 <= cs[3] < NT and 0 <= i - 3 < NT:
                    R2b(cs[3])

            if debug == "p2":
                dump("yhT", yhT_all, [128, 4, SEQ], BF16)
                raise _Stop()

            P.barrier()
            AR.release("ot2", "ot3", "hxs")
            AR.release("Sb", "Spb_all", "ot0", "ot1", "sq0", "sq1", "rr0", "rr1",
                       "sggT_all", "QbT_all", "opart_all", "bPb_all", "abcb_all")

            yfT_all = AR.tile("yfT_all", BF16, [128, 4, SEQ], hi=True)
            dslot = [AR.tile("dslot%d" % i, BF16, [128, NT, 512]) for i in range(2)]
            wslots = [AR.tile("wslot%d" % i, BF16, [128, 8, 512]) for i in range(2)]
            adab = AR.tile("adab", F32, [128, D])
            gbc = AR.tile("gbc", F32, [128, D])
            mtmp = AR.tile("mtmp", F32, [128, 512])
            DMA("sp", gbc, bcast_row(d_n2g))
            late_mods = [lambda: mods(2, h_shift(mx2, None), 7), lambda: mods(3, h_shift(B2, None), 7),
                         lambda: mods(4, h_scale(A2, None), 7), lambda: mods(5, h_shift(mx5, None), 7)]
            Wa = AR.tile("Wa", BF16, [128, 4, D])
            Wb = AR.tile("Wb", BF16, [128, 4, D])
            Pcs = [[AR.tile("Pcs%d%d" % (cs, g), BF16, [128, 512]) for g in range(4)] for cs in range(2)]
            dftv = [d.v(lambda a: a.rearrange("(lc p) k -> p lc k", p=128)) for d in (d_dftc, d_dfts)]
            Pc0 = AR.tile("Pc0", BF16, [128, 4, 2])
            for sl in range(2):
                for cs in range(2):
                    DMA("sp", dslot[cs], dftv[cs][:, :, sl * 512:(sl + 1) * 512])
                for cs in range(2):
                    if cs == 1:
                        late_mods[2 * sl]()
                        late_mods[2 * sl + 1]()
                        if sl == 0:
                            DMA("pool", Wa, d_wa.v(lambda a: a.rearrange("(g p) n -> p g n", p=128)))
                            DMA("pool", Wb, d_wb.v(lambda a: a.rearrange("(g p) n -> p g n", p=128)))
                    for g in range(4):
                        bk = banks[cs * 4 + g]
                        for lc in range(NT):
                            MM(bk, U[lc][:, g * 128:(g + 1) * 128], dslot[cs][:, lc, :], lc == 0, lc == NT - 1)
                        CP("act" if g % 2 == 0 else "dve", Pcs[cs][g], bk)
                for g in range(4):
                    bk = banks[g]
                    MM(bk, cc, Pcs[0][g], True, False)
                    MM(bk, scn, Pcs[1][g], False, True)
                    c0 = 1 + 512 * sl
                    CP("act" if g % 2 == 0 else "dve", sub(yfT_all, lambda a, g=g, c0=c0: a[:, g, c0:c0 + 512]), bk)
                    bm = banks[4 + g]
                    MM(bm, cc, Pcs[0][g], True, False)
                    MM(bm, scp, Pcs[1][g], False, True)
                    m0 = 1536 - 512 * sl
                    nm = 512 - sl
                    CP("dve" if g % 2 == 0 else "act",
                       sub(yfT_all, lambda a, g=g, m0=m0, nm=nm: a[:, g, m0 + 512 - nm:m0 + 512][:, ::-1]), bm[:, 0:nm])
            for g in range(4):
                for lc in range(NT):
                    MM(banks[g][:, 0:2], U[lc][:, g * 128:(g + 1) * 128], ones_bf[:, 0:2], lc == 0, lc == NT - 1)
                CP("dve", Pc0[:, g, :], banks[g][:, 0:2])
                MM(banks[4 + g][:, 0:2], cc, Pc0[:, g, :], True, True)
                CP("act", sub(yfT_all, lambda a, g=g: a[:, g, 0:1]), banks[4 + g][:, 0:1])
            P.barrier()
            if debug == "p2b":
                dump("yfT", yfT_all, [128, 4, SEQ], BF16)
                dump("U", U_all, [128, NT, 512], BF16)
                raise _Stop()
            AR.release("U_all", "dslot0", "dslot1", "Wf", "Pc0", *["Pcs%d%d" % (cs, g) for cs in range(2) for g in range(4)])
            AR.release("wslot0", "wslot1", "adab", "gbc", "mtmp", "ccol", "csil", "cb_bf")
            Wg = [AR.tile("Wg%d" % i, BF16, [128, 8, 512]) for i in range(4)]
            for i in (0, 2, 1, 3):
                DMA("pool", Wg[i], wview(d_win, 3072 + i * 512, 3072 + (i + 1) * 512))

            mT_all = AR.tile("mT_all", BF16, [128, 8, SEQ])
            hxTb = [AR.tile("hxTb%d" % i, BF16, [128, 8, 512]) for i in range(2)]
            sga = AR.tile("sga", F32, [128, 512])
            sgb = AR.tile("sgb", F32, [128, 512])
            m1 = AR.tile("m1", F32, [128, 512])
            m2 = AR.tile("m2", F32, [128, 512])
            mT = {}
            def prep_norm(tb, i):
                n = tb * 4 + i
                xt = xts[n % 2]
                DMA("sp", xt, d_x[n * 128:(n + 1) * 128, :])
                STT(xt, xt, rstd_c[n + 2], A1, ALU.mult, ALU.mult)
                TT("dve", hx, xt, B1, ALU.add)

            def prep_tr(tb, i):
                hb_ = hxTb[tb % 2]
                for kc in range(8):
                    TR(bankbf[0][:, kc * 128:(kc + 1) * 128], hx[:, kc * 128:(kc + 1) * 128], ident)
                CP("dve", hb_[:, :, i * 128:(i + 1) * 128], bankbf[0].v(lambda a: a.rearrange("p (a b) -> p a b", a=8)))

            for i in range(4):
                prep_norm(0, i)
                prep_tr(0, i)
            for tb in range(4):
                hb = hxTb[tb % 2]
                for dc in range(8):
                    if tb + 1 < 4 and dc % 2 == 0:
                        prep_norm(tb + 1, dc // 2)
                    s = 4 * (dc % 2)
                    kga, kgb, kya, kyb = banks[s], banks[s + 1], banks[s + 2], banks[s + 3]
                    ca = dc * 128
                    for kc in range(8):
                        MM(kga, Wg[ca // 512][:, kc, ca % 512: ca % 512 + 128], hb[:, kc, :], kc == 0, kc == 7)
                    cbk = 1024 + dc * 128
                    for kc in range(8):
                        MM(kgb, Wg[cbk // 512][:, kc, cbk % 512: cbk % 512 + 128], hb[:, kc, :], kc == 0, kc == 7)
                    for g in range(4):
                        MM(kya, Wa[:, g, dc * 128:(dc + 1) * 128], yfT_all[:, g, tb * 512:(tb + 1) * 512], g == 0, g == 3)
                    for h in range(4):
                        MM(kyb, Wb[:, h, dc * 128:(dc + 1) * 128], yhT_all[:, h, tb * 512:(tb + 1) * 512], h == 0, h == 3)
                    ACT(sga, kga, AF.Sigmoid)
                    ACT(sgb, kgb, AF.Sigmoid)
                    TT("dve", m1, sga, kya, ALU.mult)
                    TT("dve", m2, sgb, kyb, ALU.mult)
                    mT[(dc, tb)] = sub(mT_all, lambda a, dc=dc, tb=tb: a[:, dc, tb * 512:(tb + 1) * 512], "mT%d_%d" % (dc, tb))
                    TT("dve", mT[(dc, tb)], m1, m2, ALU.add)
                    if tb + 1 < 4 and dc % 2 == 1:
                        prep_tr(tb + 1, dc // 2)
            if debug == "p3a":
                P.barrier()
                dump("mT", mT_all, [128, 8, SEQ], BF16)
                raise _Stop()
            P.barrier()
            AR.release("Wg0", "Wg1", "Wg2", "Wg3", "Wa", "Wb", "hxTb0", "hxTb1", "sga", "sgb", "m1", "m2",
                       "yhT_all", "yfT_all", "A1", "B1", "hxT0", "hxT1")

            x1_all = AR.tile("x1_all", F32, [128, NT, D], hi=True)
            h2T_all = AR.tile("h2T_all", BF16, [128, 8, SEQ], hi=True)
            Wo = [AR.tile("Wo%d" % i, BF16, [128, 8, 512]) for i in range(2)]
            for i in range(2):
                DMA("pool", Wo[i], wview(d_wout, i * 512, (i + 1) * 512))
            x1 = [sub(x1_all, lambda a, n=n: a[:, n], "x1_%d" % n) for n in range(NT)]
            h2T = [sub(h2T_all, lambda a, n=n: a[:, :, n * 128:(n + 1) * 128], "h2T%d" % n) for n in range(NT)]
            junk = AR.tile("junk", BF16, [128, D])
            tmpn = AR.tile("tmpn", F32, [128, D])
            rstd2 = AR.tile("rstd2", F32, [128, 2])
            tmpx = [AR.tile("tmpx%d" % i, F32, [128, D]) for i in range(2)]

            def stageX(n):
                xt = xts[n % 2]
                DMA("sp", xt, d_x[n * 128:(n + 1) * 128, :])
                for cb in range(2):
                    bk = banks[(2 * n + cb) % 4]
                    for dc in range(8):
                        MM(bk, Reg(mT_all.ap[:, dc, n * 128:(n + 1) * 128], mT[(dc, n // 4)].bufs), Wo[cb][:, dc, :], dc == 0, dc == 7)
                    TT("dve", tmpx[n % 2][:, cb * 512:(cb + 1) * 512], bk, mx2[:, cb * 512:(cb + 1) * 512], ALU.mult)
                TT("dve", x1[n], tmpx[n % 2], xt, ALU.add)

            hxd = [hx, AR.tile("hxb", BF16, [128, D])]

            def stageY1(n):
                ACT(junk, x1[n], AF.Square, accum_out=ss[:, 1:2])
                ACT(lnv[:, 1:2], ss[:, 1:2], AF.Ln, scale=1.0 / D, bias=EPS)
                ACT(rstd2[:, 0:1], lnv[:, 1:2], AF.Exp, scale=-0.5)
                STT(tmpn, x1[n], rstd2[:, 0:1], A2, ALU.mult, ALU.mult)
                TT("dve", hxd[n % 2], tmpn, B2, ALU.add)

            def stageY2(n):
                bb = bankbf[4 + n % 2]
                for kc in range(8):
                    TR(bb[:, kc * 128:(kc + 1) * 128], hxd[n % 2][:, kc * 128:(kc + 1) * 128], ident)
                CP("act", h2T[n], bb.v(lambda a: a.rearrange("p (a b) -> p a b", a=8)))

            stageX(0)
            for n in range(NT):
                if n + 1 < NT:
                    stageX(n + 1)
                stageY1(n)
                if n >= 1:
                    stageY2(n - 1)
            stageY2(NT - 1)
            if debug == "p3b":
                P.barrier()
                dump("x1", x1_all, [128, NT, D])
                dump("h2T", h2T_all, [128, 8, SEQ], BF16)
                raise _Stop()
            Wu = [[None, None], [None, None]]
            Dg = [[None, None], [None, None]]
            for hf in range(2):
                Wu[0][hf] = AR.tile("Wu0%d" % hf, BF16, [128, 8, 128])
                Dg[0][hf] = AR.tile("Dg0%d" % hf, BF16, [128, 9, 128])
                DMA("pool", Wu[0][hf], wview(d_wup, hf * DFF, hf * DFF + 128))
                for k in range(9):
                    TS("pool", Dg[0][hf][:, k, :], ident, convw[:, hf * NFF, k:k + 1], 1.0, ALU.mult, ALU.mult)
            P.barrier()
            AR.release("Wo0", "Wo1", "mT_all", "junk", "tmpn", "tmpx0", "tmpx1", "hx", "hxb", "mx2", "A2", "B2", "xt0", "xt1")

            fg = AR.tile("fg", F32, [128, D])
            DMA("sp", fg, bcast_row(d_fg))
            GROUPS = [(0, 6), (6, 12), (12, 17), (17, 22)]
            GMAX = 6
            actT_all = AR.tile("actT", BF16, [128, GMAX, SEQ])
            Wd = AR.tile("Wd", BF16, [128, GMAX, D])
            zp = [[AR.tile("zp%d%d" % (i, hf), BF16, [128, 34, 66]) for hf in range(2)] for i in range(2)]
            zpb = [[[Buf("zpb") for tb in range(4)] for hf in range(2)] for i in range(2)]
            for hf in range(2):
                Wu[1][hf] = AR.tile("Wu1%d" % hf, BF16, [128, 8, 128])
                Dg[1][hf] = AR.tile("Dg1%d" % hf, BF16, [128, 9, 128])
            s1 = [AR.tile("s1_%d" % i, BF16, [128, 512]) for i in range(2)]
            tmpd = AR.tile("tmpd", F32, [128, 512])
            yo = [AR.tile("yo%d" % i, F32, [128, D]) for i in range(2)]
            junk = AR.tile("junk", BF16, [128, D])
            for i in range(2):
                for hf in range(2):
                    MEMSET("pool", Reg(zp[i][hf].ap, zp[i][hf].bufs + zpb[i][hf]), 0.0)

            def up(j):
                i = j % 2
                for hf in range(2):
                    if j == 0:
                        continue
                    col0 = hf * DFF + j * 128
                    DMA("pool", Wu[i][hf], wview(d_wup, col0, col0 + 128))
                    for k in range(9):
                        TS("pool", Dg[i][hf][:, k, :], ident, convw[:, hf * NFF + j, k:k + 1], 1.0, ALU.mult, ALU.mult)
                for hf in range(2):
                    for tb in range(4):
                        bk = banks[(hf * 4 + tb) % 2]
                        for kc in range(8):
                            MM(bk, Wu[i][hf][:, kc, :], h2T_all[:, kc, tb * 512:(tb + 1) * 512], kc == 0, kc == 7)
                        dst = Reg(zp[i][hf].ap[:, 1 + tb * 8: 9 + tb * 8, 1:65], [zpb[i][hf][tb]])
                        CP("act" if tb % 2 == 0 else "dve", dst, bk.v(lambda a: a.rearrange("p (r w) -> p r w", r=8)))

            def conv(j, jl):
                i = j % 2
                for tb in range(4):
                    nb = [t for t in (tb - 1, tb, tb + 1) if 0 <= t < 4]
                    kc1 = banks[2 + tb % 2]
                    kc2 = banks[4 + tb % 2]
                    for hf, bk in ((0, kc1), (1, kc2)):
                        src_bufs = zp[i][hf].bufs + [zpb[i][hf][t] for t in nb]
                        for k in range(9):
                            dr, dw = k // 3, k % 3
                            rhs = Reg(zp[i][hf].ap[:, tb * 8 + dr: tb * 8 + dr + 8, dw: dw + 64], src_bufs)
                            MM(bk, Dg[i][hf][:, k, :], rhs, k == 0, k == 8)
                    ACT(s1[tb % 2], kc1, AF.Silu, bias=convb[:, j:j + 1])
                    STT(Reg(actT_all.ap[:, jl, tb * 512:(tb + 1) * 512], actT_bufs[jl]), kc2, convb[:, NFF + j:NFF + j + 1], s1[tb % 2], ALU.add, ALU.mult)

            actT_bufs = [[Buf("actT%d" % g)] for g in range(GMAX)]
            up(0)
            for (g0, g1) in GROUPS:
                for j in range(g0, g1):
                    if j + 1 < NFF:
                        up(j + 1)
                    conv(j, j - g0)
                    if j == g0 + 1:
                        DMA("pool", Wd[:, 0:g1 - g0, :], d_wdown[g0 * 128:g1 * 128, :].v(lambda a: a.rearrange("(j p) n -> p j n", p=128)))
                        TT("pool", Wd[:, 0:g1 - g0, :], Wd[:, 0:g1 - g0, :], mx5.v(lambda a, g=g1 - g0: a.unsqueeze(1).to_broadcast([128, g, D])), ALU.mult)
                for n in range(NT):
                    for cb in range(2):
                        bk = banks[(6, 7, 2, 3, 4, 5)[(2 * n + cb) % 6]]
                        for jl in range(g1 - g0):
                            MM(bk, Reg(actT_all.ap[:, jl, n * 128:(n + 1) * 128], actT_bufs[jl]), Wd[:, jl, cb * 512:(cb + 1) * 512], jl == 0, jl == g1 - g0 - 1)
                        TT("dve", x1[n][:, cb * 512:(cb + 1) * 512], bk, x1[n][:, cb * 512:(cb + 1) * 512], ALU.add)
                    if g1 == NFF:
                        ACT(junk, x1[n], AF.Square, accum_out=ss[:, 0:1])
                        ACT(lnv[:, 0:1], ss[:, 0:1], AF.Ln, scale=1.0 / D, bias=EPS)
                        ACT(rstd2[:, 1:2], lnv[:, 0:1], AF.Exp, scale=-0.5)
                        STT(yo[n % 2], x1[n], rstd2[:, 1:2], fg, ALU.mult, ALU.mult)
                        DMA("sp", d_out[n * 128:(n + 1) * 128, :], yo[n % 2])


        try:
            body()
        except _Stop:
            pass
        P = hold['P']
        AR = hold['AR']
        P.emit()
        build.stats = dict(P.stats, arena_peak=AR.peak)
    return nc, dbg_out


def _consts():
    bf = ml_dtypes.bfloat16
    s = np.arange(128)[:, None]
    t = np.arange(128)[None, :]
    c = {}
    c["ident"] = np.eye(128, dtype=np.float32).astype(bf)
    c["trimf"] = ((s <= t).astype(np.float32) - (s <= 63).astype(np.float32))
    c["trimb"] = ((s >= t).astype(np.float32) - (s >= 64).astype(np.float32))
    c["maskf"] = np.tile((s <= t).astype(np.float32), (1, 4)).astype(bf)
    c["maskb"] = np.tile((s >= t).astype(np.float32), (1, 4)).astype(bf)
    sv = np.arange(128)
    c["self"] = np.stack([np.ones(128), sv <= 63, sv > 63, np.zeros(128)], axis=1).astype(np.float32)
    c["selb"] = np.stack([np.ones(128), sv >= 64, sv < 64, np.zeros(128)], axis=1).astype(np.float32)
    c["ones_bf"] = np.ones((128, 128), np.float32).astype(bf)
    ang = 2.0 * np.pi * ((s * t) % 128) / 128.0
    c["cc"] = (np.cos(ang) / 512.0).astype(np.float32).astype(bf)
    c["scn"] = (-np.sin(ang) / 512.0).astype(np.float32).astype(bf)
    c["scp"] = (np.sin(ang) / 512.0).astype(np.float32).astype(bf)
    l = np.arange(SEQ, dtype=np.int64)
    kk = np.arange(1, SEQ // 2 + 1, dtype=np.int64)
    angL = 2.0 * np.pi * ((l[:, None] * kk[None, :]) % SEQ) / SEQ
    c["dftc"] = np.cos(angL).astype(np.float32).astype(bf)
    c["dfts"] = np.sin(angL).astype(np.float32).astype(bf)
    return c


_CACHE = {}


def kernel(x, c, ctx, c_ctx, ada_w, ada_b, norm1_g, w_in, hg_lb, hg_onorm_g, w_a, w_b, w_out,
           norm2_g, ffn_up, ffn_conv_w, ffn_conv_b, ffn_down, final_g, _debug=None):
    f = lambda a: np.ascontiguousarray(np.asarray(a, dtype=np.float32))
    x = f(x); c = f(c); ctx = f(ctx); c_ctx = f(c_ctx)
    key = _debug
    if key not in _CACHE:
        _CACHE[key] = (build(_debug), _consts())
    (nc, dbg_out), consts = _CACHE[key]
    shared = dict(
        ada_w=f(ada_w)[0], ada_b=f(ada_b)[0], norm1_g=f(norm1_g)[0], w_in=f(w_in)[0], hg_lb=f(hg_lb),
        og=np.ascontiguousarray(f(hg_onorm_g)[0].reshape(4, 128).T),
        og_row=f(hg_onorm_g)[0],
        w_a=f(w_a)[0], w_b=f(w_b)[0], w_out=f(w_out)[0], norm2_g=f(norm2_g)[0], ffn_up=f(ffn_up)[0],
        convw=np.ascontiguousarray(f(ffn_conv_w)[0].reshape(9, 2 * NFF, 128).transpose(2, 1, 0)),
        convb=np.ascontiguousarray(f(ffn_conv_b)[0].reshape(2 * NFF, 128).T),
        ffn_down=f(ffn_down)[0], final_g=f(final_g),
    )
    shared.update(consts)
    ccx = c_ctx.reshape(8, 128).T
    in_maps = []
    for b in range(8):
        m = dict(shared)
        m["x"] = x[b]
        m["ctx"] = ctx[b]
        m["ccol"] = np.ascontiguousarray(np.concatenate([c[b].reshape(8, 128).T, ccx], axis=1))
        in_maps.append(m)
    res = run_bass_kernel_spmd(nc, in_maps, core_ids=list(range(8)))
    out = np.stack([np.asarray(res.results[b]["out"], dtype=np.float32) for b in range(8)], axis=0)
    if _debug is not None:
        return out, res.results
    return out
```
